# Optimizing a Trainium2 kernel written in Bass

```python
import math
import functools
import jax
import jax.numpy as jnp
from jax import lax
import numpy as np

D_MODEL = 1024
BATCH = 2
SEQ = 8192
DEPTH = 1
DEC_BATCH = 128
DEC_SEQ = 4
PAST_LEN = 2048
PAGE_SIZE = 128

DA_DK = 64
DA_DV = 2 * DA_DK
DA_HEADS = D_MODEL // DA_DV
DA_QK_W = 2 * DA_HEADS * DA_DK
DA_V_W = DA_HEADS * DA_DV
RET_HEADS = 4
RET_DK = D_MODEL // RET_HEADS
RET_DV = D_MODEL // RET_HEADS
RET_QK_W = RET_HEADS * RET_DK
RET_V_W = RET_HEADS * RET_DV
RET_CHUNK = 128
RET_THETA = 10000.0
IN_WIDTHS = (DA_QK_W, DA_QK_W, DA_V_W, RET_QK_W, RET_QK_W, RET_V_W, RET_V_W, D_MODEL, D_MODEL)
W_IN_COLS = sum(IN_WIDTHS)
Q_BLOCK = 128
D_FF = 2816
HALF_STEP = 0.5
ROPE_THETA = 10000.0
NORM_EPS = 1e-6
SUBLN_EPS = 1e-5
GN_EPS = 1e-5
N_MOD = 9
NEG_INF = -1e30

kernel_name = 'hybrid_diffattn_retention_macaron_step'


def rms_norm(x, g, eps=NORM_EPS):
    xf = x.astype(jnp.float32)
    y = xf * lax.rsqrt(jnp.mean(xf * xf, axis=-1, keepdims=True) + eps)
    return (y * g.astype(jnp.float32)).astype(x.dtype)


def group_norm_heads(x, g, eps=GN_EPS):
    xf = x.astype(jnp.float32)
    mu = jnp.mean(xf, axis=-1, keepdims=True)
    var = jnp.mean(jnp.square(xf - mu), axis=-1, keepdims=True)
    y = (xf - mu) * lax.rsqrt(var + eps)
    return y * g.reshape(x.shape[-2], x.shape[-1]).astype(jnp.float32)


def modulate(x, g, shift, scale):
    return rms_norm(x, g) * (1 + scale[:, None, :]) + shift[:, None, :]


def swiglu_ffn(h, w_in, w_out):
    a, b = jnp.split(h @ w_in, 2, axis=-1)
    return (jax.nn.silu(a) * b) @ w_out


def rope_half(x, pos):
    half = x.shape[-1] // 2
    inv = ROPE_THETA ** (-jnp.arange(half, dtype=jnp.float32) / half)
    ang = pos.astype(jnp.float32)[:, None] * inv[None, :]
    cos = jnp.cos(ang)[None, :, None, :]
    sin = jnp.sin(ang)[None, :, None, :]
    xf = x.astype(jnp.float32)
    x1, x2 = xf[..., :half], xf[..., half:]
    return jnp.concatenate([x1 * cos - x2 * sin, x2 * cos + x1 * sin], axis=-1).astype(x.dtype)


def retnet_rotate(x, pos):
    half = x.shape[-1] // 2
    angle = 1.0 / (RET_THETA ** jnp.linspace(0.0, 1.0, half, dtype=jnp.float32))
    ang = pos.astype(jnp.float32)[:, None] * angle[None, :]
    cos = jnp.cos(ang)[None, :, None, :]
    sin = jnp.sin(ang)[None, :, None, :]
    xf = x.astype(jnp.float32).reshape(*x.shape[:-1], half, 2)
    x0, x1 = xf[..., 0], xf[..., 1]
    out = jnp.stack([x0 * cos - x1 * sin, x1 * cos + x0 * sin], axis=-1).reshape(x.shape)
    return out.astype(x.dtype)


def retention_log_decay():
    return jnp.log1p(-jnp.exp2(-5.0 - jnp.arange(RET_HEADS, dtype=jnp.float32)))


def retention_chunk(q, k, v, state, log_g):
    L = q.shape[1]
    qf, kf, vf = q.astype(jnp.float32), k.astype(jnp.float32), v.astype(jnp.float32)
    idx = jnp.arange(L, dtype=jnp.float32)
    dist = idx[:, None] - idx[None, :]
    decay = jnp.where(dist[None] >= 0, jnp.exp(jnp.maximum(dist, 0.0)[None] * log_g[:, None, None]), 0.0)
    scores = jnp.einsum('blhd,bshd->bhls', qf, kf) * decay[None]
    inner = jnp.einsum('bhls,bshv->blhv', scores, vf)
    q_decay = jnp.exp((idx + 1.0)[:, None] * log_g[None, :])
    cross = jnp.einsum('blhd,bhdv->blhv', qf, state) * q_decay[None, :, :, None]
    k_decay = jnp.exp((L - 1.0 - idx)[:, None] * log_g[None, :])
    new_state = (jnp.exp(L * log_g)[None, :, None, None] * state
                 + jnp.einsum('bshd,bshv->bhdv', kf * k_decay[None, :, :, None], vf))
    return inner + cross, new_state


def retention_prompt(q, k, v, log_g):
    B, T = q.shape[:2]
    n_chunks = T // RET_CHUNK

    def chunks(a):
        return a.reshape(B, n_chunks, RET_CHUNK, *a.shape[2:]).swapaxes(0, 1)

    state0 = jnp.zeros((B, RET_HEADS, RET_DK, RET_DV), jnp.float32)

    def step(state, qkv):
        o, state = retention_chunk(*qkv, state, log_g)
        return state, o

    state, o = lax.scan(step, state0, (chunks(q), chunks(k), chunks(v)))
    return o.swapaxes(0, 1).reshape(B, T, RET_HEADS, RET_DV), state


def diff_attend(q, k, v, q_pos, k_pos, lam):
    B, Lq = q.shape[:2]
    Lk = k.shape[1]
    qh = q.reshape(B, Lq, DA_HEADS, 2, DA_DK)
    kh = k.reshape(B, Lk, DA_HEADS, 2, DA_DK)
    s = jnp.einsum('bqhcd,bkhcd->bhcqk', qh, kh).astype(jnp.float32)
    mask = k_pos[None, :] <= q_pos[:, None]
    s = jnp.where(mask, s, jnp.float32(NEG_INF))
    p = jax.nn.softmax(s, axis=-1)
    a = p[:, :, 0] - lam * p[:, :, 1]
    return jnp.einsum('bhqk,bkhv->bqhv', a.astype(v.dtype), v)


def diff_attn_prompt(q, k, v, lam):
    B, T = q.shape[:2]
    k_pos = jnp.arange(T)

    def block(i):
        start = i * Q_BLOCK
        qb = lax.dynamic_slice_in_dim(q, start, Q_BLOCK, axis=1)
        return diff_attend(qb, k, v, start + jnp.arange(Q_BLOCK), k_pos, lam)

    o = lax.map(block, jnp.arange(T // Q_BLOCK))
    return o.swapaxes(0, 1).reshape(B, T, DA_HEADS, DA_DV)


def diff_attn_cached(q, k, v, lam, past_k, past_v, q_pos):
    k_all = jnp.concatenate([past_k.astype(k.dtype), k], axis=1)
    v_all = jnp.concatenate([past_v.astype(v.dtype), v], axis=1)
    k_pos = jnp.arange(k_all.shape[1])
    return diff_attend(q, k_all, v_all, q_pos, k_pos, lam)


def token_mixer(h, pos, lam_init, attn_fn, ret_fn, w_in, w_out, lam_q1, lam_k1, lam_q2, lam_k2,
                subln_g, ret_norm_g):
    B, T, _ = h.shape
    split_at = [int(s) for s in np.cumsum(IN_WIDTHS)[:-1]]
    qa, ka, va, qr, kr, vr, gr, ga, gb = jnp.split(h @ w_in, split_at, axis=-1)
    qa = rope_half(qa.reshape(B, T, 2 * DA_HEADS, DA_DK), pos) * (DA_DK ** -0.5)
    ka = rope_half(ka.reshape(B, T, 2 * DA_HEADS, DA_DK), pos)
    va = va.reshape(B, T, DA_HEADS, DA_DV)
    lam = (jnp.exp(jnp.sum(lam_q1.astype(jnp.float32) * lam_k1.astype(jnp.float32)))
           - jnp.exp(jnp.sum(lam_q2.astype(jnp.float32) * lam_k2.astype(jnp.float32))) + lam_init)
    o_a = attn_fn(qa, ka, va, lam)
    o_a = (rms_norm(o_a, subln_g, SUBLN_EPS) * (1.0 - lam_init)).reshape(B, T, D_MODEL)
    qr = retnet_rotate(qr.reshape(B, T, RET_HEADS, RET_DK), pos)
    kr = retnet_rotate(kr.reshape(B, T, RET_HEADS, RET_DK), pos) * (RET_DK ** -0.5)
    vr = vr.reshape(B, T, RET_HEADS, RET_DV)
    o_r, ret_state = ret_fn(qr, kr, vr)
    o_r = group_norm_heads(o_r, ret_norm_g).reshape(B, T, D_MODEL).astype(h.dtype) * jax.nn.silu(gr)
    merged = jax.nn.sigmoid(ga) * o_a + jax.nn.sigmoid(gb) * o_r
    return merged @ w_out, (ka, va, ret_state)


def decoder_layer(x, c, pos, lam_init, attn_fn, ret_fn, ada_w, ada_b, norm_ffn1, norm_mix, norm_ffn2,
                  ffn1_w_in, ffn1_w_out, ffn2_w_in, ffn2_w_out, w_in, w_out, lam_q1, lam_k1, lam_q2,
                  lam_k2, subln_g, ret_norm_g):
    mod = jax.nn.silu(c) @ ada_w + ada_b
    sh1, sc1, gt1, sh2, sc2, gt2, sh3, sc3, gt3 = jnp.split(mod, N_MOD, axis=-1)
    h = modulate(x, norm_ffn1, sh1, sc1)
    x = x + HALF_STEP * gt1[:, None, :] * swiglu_ffn(h, ffn1_w_in, ffn1_w_out)
    h = modulate(x, norm_mix, sh2, sc2)
    m, new_state = token_mixer(h, pos, lam_init, attn_fn, ret_fn, w_in, w_out, lam_q1, lam_k1,
                               lam_q2, lam_k2, subln_g, ret_norm_g)
    x = x + gt2[:, None, :] * m
    h = modulate(x, norm_ffn2, sh3, sc3)
    x = x + HALF_STEP * gt3[:, None, :] * swiglu_ffn(h, ffn2_w_in, ffn2_w_out)
    return x, new_state


def setup_inputs(seed: int = 0) -> dict:
    key = jax.random.key(seed)
    ks = jax.random.split(key, 32)
    f32 = jnp.float32

    def w(k, shape, fan_in):
        return jax.random.normal(k, shape, f32) * (fan_in ** -0.5)

    def gain(k, shape):
        return 1.0 + 0.05 * jax.random.normal(k, shape, f32)

    n_pages = PAST_LEN // PAGE_SIZE
    n_used = DEC_BATCH * n_pages
    n_pool = n_used + (n_used + 3) // 4
    page_table = jax.random.permutation(ks[0], n_pool)[:n_used].reshape(DEC_BATCH, n_pages).astype(jnp.int32)
    return {
        'x_prompt': jax.random.normal(ks[1], (BATCH, SEQ, D_MODEL), f32),
        'x_sample': jax.random.normal(ks[2], (DEC_BATCH, DEC_SEQ, D_MODEL), f32),
        'cache_k': jax.random.normal(ks[3], (DEPTH, n_pool, PAGE_SIZE, 2 * DA_HEADS, DA_DK), f32),
        'cache_v': jax.random.normal(ks[4], (DEPTH, n_pool, PAGE_SIZE, DA_HEADS, DA_DV), f32),
        'state_ret': 0.5 * jax.random.normal(ks[5], (DEPTH, DEC_BATCH, RET_HEADS, RET_DK, RET_DV), f32),
        'page_table': page_table,
        'c_prompt': jax.random.normal(ks[6], (BATCH, D_MODEL), f32),
        'c_sample': jax.random.normal(ks[7], (DEC_BATCH, D_MODEL), f32),
        'ada_w': w(ks[8], (DEPTH, D_MODEL, N_MOD * D_MODEL), D_MODEL),
        'ada_b': 0.01 * jax.random.normal(ks[9], (DEPTH, N_MOD * D_MODEL), f32),
        'norm_ffn1': gain(ks[10], (DEPTH, D_MODEL)),
        'norm_mix': gain(ks[11], (DEPTH, D_MODEL)),
        'norm_ffn2': gain(ks[12], (DEPTH, D_MODEL)),
        'ffn1_w_in': w(ks[13], (DEPTH, D_MODEL, 2 * D_FF), D_MODEL),
        'ffn1_w_out': w(ks[14], (DEPTH, D_FF, D_MODEL), D_FF),
        'ffn2_w_in': w(ks[15], (DEPTH, D_MODEL, 2 * D_FF), D_MODEL),
        'ffn2_w_out': w(ks[16], (DEPTH, D_FF, D_MODEL), D_FF),
        'w_in': w(ks[17], (DEPTH, D_MODEL, W_IN_COLS), D_MODEL),
        'w_out': w(ks[18], (DEPTH, D_MODEL, D_MODEL), D_MODEL),
        'lam_q1': 0.1 * jax.random.normal(ks[19], (DEPTH, DA_DK), f32),
        'lam_k1': 0.1 * jax.random.normal(ks[20], (DEPTH, DA_DK), f32),
        'lam_q2': 0.1 * jax.random.normal(ks[21], (DEPTH, DA_DK), f32),
        'lam_k2': 0.1 * jax.random.normal(ks[22], (DEPTH, DA_DK), f32),
        'subln_g': gain(ks[23], (DEPTH, DA_DV)),
        'ret_norm_g': gain(ks[24], (DEPTH, D_MODEL)),
        'norm_final': gain(ks[25], (D_MODEL,)),
    }


def reference(x_prompt, x_sample, cache_k, cache_v, state_ret, page_table, c_prompt, c_sample,
              ada_w, ada_b, norm_ffn1, norm_mix, norm_ffn2, ffn1_w_in, ffn1_w_out, ffn2_w_in,
              ffn2_w_out, w_in, w_out, lam_q1, lam_k1, lam_q2, lam_k2, subln_g, ret_norm_g,
              norm_final):
    seq = x_prompt.shape[1]
    dec_batch, dec_seq = x_sample.shape[:2]
    past_len = page_table.shape[1] * cache_k.shape[2]
    pos_prompt = jnp.arange(seq)
    pos_sample = past_len + jnp.arange(dec_seq)
    log_g = retention_log_decay()
    ret_prompt_fn = functools.partial(retention_prompt, log_g=log_g)

    yp, ys = x_prompt, x_sample
    kp_l, vp_l, sp_l, ks_l, vs_l, ss_l = [], [], [], [], [], []
    for l in range(DEPTH):
        lam_init = 0.8 - 0.6 * math.exp(-0.3 * l)
        weights = (ada_w[l], ada_b[l], norm_ffn1[l], norm_mix[l], norm_ffn2[l], ffn1_w_in[l],
                   ffn1_w_out[l], ffn2_w_in[l], ffn2_w_out[l], w_in[l], w_out[l], lam_q1[l],
                   lam_k1[l], lam_q2[l], lam_k2[l], subln_g[l], ret_norm_g[l])
        yp, (kp, vp, sp) = decoder_layer(yp, c_prompt, pos_prompt, lam_init, diff_attn_prompt,
                                         ret_prompt_fn, *weights)
        past_k = cache_k[l][page_table].reshape(dec_batch, past_len, 2 * DA_HEADS, DA_DK)
        past_v = cache_v[l][page_table].reshape(dec_batch, past_len, DA_HEADS, DA_DV)
        attn_s = functools.partial(diff_attn_cached, past_k=past_k, past_v=past_v, q_pos=pos_sample)
        ret_s = functools.partial(retention_chunk, state=state_ret[l].astype(jnp.float32), log_g=log_g)
        ys, (ksm, vsm, ssm) = decoder_layer(ys, c_sample, pos_sample, lam_init, attn_s, ret_s, *weights)
        kp_l.append(kp)
        vp_l.append(vp)
        sp_l.append(sp)
        ks_l.append(ksm)
        vs_l.append(vsm)
        ss_l.append(ssm)

    y_prompt = rms_norm(yp, norm_final)
    y_sample = rms_norm(ys, norm_final)
    k_prompt = jnp.stack(kp_l)
    v_prompt = jnp.stack(vp_l)
    state_ret_prompt = jnp.stack(sp_l)
    k_sample = jnp.stack(ks_l)
    v_sample = jnp.stack(vs_l)
    state_ret_sample = jnp.stack(ss_l)
    return (y_prompt, y_sample, k_prompt, v_prompt, state_ret_prompt, k_sample, v_sample, state_ret_sample)
```

```python
import math
from contextlib import ExitStack
from types import SimpleNamespace
import numpy as np
import concourse.bass as bass
import concourse.mybir as mybir
from concourse.bass_utils import run_bass_kernel_spmd

F32 = mybir.dt.float32
BF16 = mybir.dt.bfloat16
I32 = mybir.dt.int32
AF = mybir.ActivationFunctionType
ALU = mybir.AluOpType
AX = mybir.AxisListType

D = 1024
DFF = 2816
NPOOL_FULL = 2560
NPT = 2048
NST = 64
NTOK = NPT + NST
import os
DBG_KINDS = set(int(c) for c in os.environ.get('DBG_KINDS', '012345678'))
DBG_GS = int(os.environ.get('DBG_GS', '1'))
DO_ATTN = int(os.environ.get('DO_ATTN', '1'))
DO_RET = int(os.environ.get('DO_RET', '1'))
DO_SATT = int(os.environ.get('DO_SATT', '1'))
OPT_PIPE = int(os.environ.get('OPT_PIPE', '1'))
OPT_PROG = int(os.environ.get('OPT_PROG', '0'))
OPT_WC = int(os.environ.get('OPT_WC', '1'))
DBG_SA = int(os.environ.get('DBG_SA', '9'))
DBG_SV = int(os.environ.get('DBG_SV', '0'))
LAM_INIT = 0.8 - 0.6 * math.exp(-0.3 * 0)
GAM = [1.0 - 2.0 ** (-5 - h) for h in range(4)]


class Buf:
    __slots__ = ("name", "w", "r", "excl")

    def __init__(self, name="", excl=False):
        self.name = name
        self.w = {}
        self.r = {}
        self.excl = excl


class Eng:
    def __init__(self, k, key, eng, is_pe=False):
        self.key = key
        self.e = eng
        self.sem = k.nc.alloc_semaphore("sem_" + key)
        self.count = 0
        self.seen = {}
        self.is_pe = is_pe
        self.dsems = []
        self.dvals = []
        self.dnext = 0


class K:
    NDMA = 12

    def __init__(self, nc):
        self.nc = nc
        self.engs = {}
        for key, eng, pe in (("pe", nc.tensor, True), ("dve", nc.vector, False),
                             ("act", nc.scalar, False), ("pool", nc.gpsimd, False),
                             ("sp", nc.sync, False)):
            self.engs[key] = Eng(self, key, eng, pe)
        self.semobj = {key: e.sem for key, e in self.engs.items()}
        self.n_wait = 0
        self.n_inst = 0

    def _deps(self, reads, writes):
        deps = {}
        for b in reads:
            for kk, v in b.w.items():
                if deps.get(kk, 0) < v:
                    deps[kk] = v
        for b in writes:
            for kk, v in b.w.items():
                if deps.get(kk, 0) < v:
                    deps[kk] = v
            for kk, v in b.r.items():
                if deps.get(kk, 0) < v:
                    deps[kk] = v
        return deps

    def _wait(self, E, deps):
        for kk, v in deps.items():
            if kk == E.key and E.is_pe:
                continue
            if E.seen.get(kk, 0) >= v:
                continue
            E.e.wait_ge(self.semobj[kk], v)
            E.seen[kk] = v
            self.n_wait += 1

    def _mark(self, key, val, reads, writes):
        for b in reads:
            if b.r.get(key, 0) < val:
                b.r[key] = val
        for b in writes:
            b.w = {key: val}
            b.r = {}

    def op(self, ek, fn, reads=(), writes=()):
        E = self.engs[ek]
        if any(b.excl for b in reads):
            writes = list(writes) + [b for b in reads if b.excl]
            reads = [b for b in reads if not b.excl]
        self._wait(E, self._deps(reads, writes))
        inst = fn(E.e)
        E.count += 1
        inst.then_inc(E.sem, 1)
        self._mark(E.key, E.count, reads, writes)
        self.n_inst += 1
        return inst

    def dma(self, qk, out, in_, reads=(), writes=(), fn=None, inc=16):
        E = self.engs[qk]
        if not E.dsems:
            for i in range(self.NDMA):
                s = self.nc.alloc_semaphore("dsem_%s_%d" % (qk, i))
                E.dsems.append(s)
                E.dvals.append(0)
                self.semobj[("d", qk, i)] = s
        self._wait(E, self._deps(reads, writes))
        i = E.dnext
        E.dnext = (i + 1) % self.NDMA
        key = ("d", qk, i)
        if E.dvals[i] > 0 and E.seen.get(key, 0) < E.dvals[i]:
            E.e.wait_ge(E.dsems[i], E.dvals[i])
            E.seen[key] = E.dvals[i]
            self.n_wait += 1
        if fn is None:
            inst = E.e.dma_start(out=out, in_=in_)
        else:
            inst = fn(E.e)
        E.dvals[i] += inc
        inst.then_inc(E.dsems[i], inc)
        self._mark(key, E.dvals[i], reads, writes)
        self.n_inst += 1
        return inst

    def coll(self, fn, reads=(), writes=()):
        E = self.engs["pool"]
        self._wait(E, self._deps(reads, writes))
        n = getattr(self, "_ncoll", 0)
        self._ncoll = n + 1
        sem = self.nc.alloc_semaphore("csem_%d" % n)
        key = ("c", n)
        self.semobj[key] = sem
        inst = fn(E.e)
        inst.then_inc(sem)
        self._mark(key, 1, reads, writes)
        self.n_inst += 1
        return inst

    def barrier(self):
        deps = {}
        for key, e in self.engs.items():
            if e.count:
                deps[key] = e.count
            for i, v in enumerate(e.dvals):
                if v:
                    deps[("d", key, i)] = v
        for key, E in self.engs.items():
            self._wait(E, dict(deps))

    def finish_all(self):
        E = self.engs["sp"]
        deps = {}
        for key, e in self.engs.items():
            if e.count:
                deps[key] = e.count
            for i, v in enumerate(e.dvals):
                if v:
                    deps[("d", key, i)] = v
        self._wait(E, deps)


class Rot:
    def __init__(self, items, share=None):
        self.items = items
        self.st = share.st if share is not None else [0]

    def next(self):
        i = self.st[0]
        it = self.items[i]
        self.st[0] = (i + 1) % len(self.items)
        return it


def build(npool=NPOOL_FULL, upto=99):
    nc = bass.Bass("TRN2", target_bir_lowering=False)
    k = K(nc)
    es = ExitStack()

    def din(name, shape, dt=F32):
        return nc.dram_tensor(name, list(shape), dt, kind="ExternalInput").ap()

    def dout(name, shape, dt=F32):
        return nc.dram_tensor(name, list(shape), dt, kind="ExternalOutput").ap()

    def dscr(name, shape, dt):
        return nc.dram_tensor(name, list(shape), dt).ap()

    def sb(name, shape, dt):
        return es.enter_context(nc.sbuf_tensor(name, list(shape), dt))

    def ps(name, shape, dt):
        return es.enter_context(nc.psum_tensor(name, list(shape), dt))

    xp = din("xp", [NPT, D])
    xs = din("xs", [NST, D])
    ctok = din("ctok", [192, D])
    ada_w = din("ada_w", [D, 9 * D])
    ada_b = din("ada_b", [1, 9 * D])
    nrm = din("nrm", [5, D])
    f1_in = din("ffn1_w_in", [D, 2 * DFF])
    f1_out = din("ffn1_w_out", [DFF, D])
    f2_in = din("ffn2_w_in", [D, 2 * DFF])
    f2_out = din("ffn2_w_out", [DFF, D])
    w_in = din("w_in", [D, 9 * D])
    w_out = din("w_out", [D, D])
    ident_d = din("ident", [128, 128])
    ropeq = din("ropeq", [NTOK, 2, 64])
    ropek = din("ropek", [NTOK, 2, 64])
    retq = din("retq", [NTOK, 2, 256])
    retk = din("retk", [NTOK, 2, 256])
    dec = din("dec", [NTOK, 8])
    lam_d = din("lam", [4, 64])
    subln_d = din("subln", [1, 128])
    msk_d = din("msk", [128, 4])
    dmp_d = din("dmp", [128, 4, 128])
    dms_d = din("dms", [64, 4, 64])
    seqm_d = din("seqm", [64, 16])
    maskd = din("maskd", [128, 16, 512])
    st_in = din("st_in", [16, 4, 256, 256])
    pt_d = din("pt", [1, 256], I32)
    satab_d = din("satab", [64, 8 * 128 + 8])
    tri_d = din("tri", [4, 64])
    cache_k2 = din("cache_k", [npool * 128, D])
    cache_v2 = din("cache_v", [npool * 128, D])

    y_p = dout("y_p", [NPT, D])
    y_s = dout("y_s", [NST, D])
    k_p = dout("k_p", [NPT, D])
    v_p = dout("v_p", [NPT, D])
    st_p = dout("st_p", [4, 256, 256])
    k_s = dout("k_s", [NST, D])
    v_s = dout("v_s", [NST, D])
    st_s = dout("st_s", [16, 4, 256, 256])

    qT_s = dscr("qT_s", [8, 128, NTOK], BF16)
    kT_x = [dscr("kT_x%d" % i, [1024, 512], BF16) for i in range(4)]
    kTs_s = dscr("kTs_s", [8, 128, NST], BF16)
    v_x = dscr("v_x", [NPT, D], BF16)
    vs_s = dscr("vs_s", [NST, D], BF16)
    qrT_s = dscr("qrT_s", [8, 128, NTOK], BF16)
    krT_s = dscr("krT_s", [8, 128, NTOK], BF16)
    krd_s = dscr("krd_s", [NTOK, D], BF16)
    vr_s = dscr("vr_s", [NTOK, D], BF16)
    gate_s = dscr("gate_s", [NTOK, 3 * D], F32)
    xmid_s = dscr("xmid_s", [NTOK, D], F32)
    E_x = dscr("E_x", [4, 128, 2048], F32)
    kT_g = [dscr("kT_g%d" % i, [4 * 1024, 512], BF16) for i in range(4)]
    v_g = [dscr("v_g%d" % i, [4 * 512, D], BF16) for i in range(4)]
    E_g = [dscr("E_g%d" % i, [4 * 128, 2048], F32) for i in range(4)]
    oa_s = dscr("oa_s", [NTOK, D], F32)
    or_s = dscr("or_s", [NTOK, D], F32)
    attn_s = dscr("attn_s", [NST, D], F32)
    bx_kT = [Buf() for _ in range(4)]
    bx_v = [Buf() for _ in range(4)]
    bx_E = [Buf() for _ in range(4)]
    bufs_d = {n: Buf(n) for n in ("qT", "kT", "kTs", "v", "vs", "qrT", "krT", "krd", "vr", "gate", "xmid", "E", "out", "oa", "or", "attn")}

    with es:
        idf = sb("idf", [128, 128], F32)
        idb = sb("idb", [128, 128], BF16)
        b_id = Buf()
        k.dma("sp", idf[:], ident_d, writes=[b_id])
        k.op("dve", lambda e: e.tensor_copy(out=idb[:], in_=idf[:]), reads=[b_id], writes=[b_id])
        eps_t = sb("eps_t", [128, 1], F32)
        eps5_t = sb("eps5_t", [128, 1], F32)
        ones_b = sb("ones_b", [128, 1], BF16)
        b_eps = Buf()

        mod = sb("mod", [128, 9 * D], F32)
        b_mod = Buf("mod")

        pmm = Rot([(ps("pmm_%d" % i, [128, 512], F32), Buf(excl=True)) for i in range(4)])
        ptr = Rot([(ps("ptr_%d" % i, [128, 8, 128], BF16), Buf(excl=True)) for i in range(2)])
        pl = [(ps("pl_%d" % i, [128, 512], F32), Buf(excl=True)) for i in range(2)]
        TB = 4
        W = SimpleNamespace()

        def mk_work(es_, tag, phase_a):
            def sbw(name, shape, dt):
                return es_.enter_context(nc.sbuf_tensor(name + tag, list(shape), dt))
            _wk = [(sbw("wk_%d" % i, [128, 5632], BF16), Buf()) for i in range(3)]
            W.wk8 = Rot([(t[:, 0:4096].rearrange("p (c n) -> p c n", n=512), b) for (t, b) in _wk])
            W.wk22 = Rot([(t[:, :].rearrange("p (c n) -> p c n", n=256), b) for (t, b) in _wk], share=W.wk8)
            W.xb = sbw("xb", [128, TB, D], F32)
            W.b_x = [Buf() for _ in range(TB)]
            W.hT = sbw("hT", [128, 8, TB * 128], BF16)
            W.b_hT = [Buf() for _ in range(TB)]
            W.gtok = sbw("gtok", [128, TB, DFF], BF16)
            W.b_g = [Buf() for _ in range(TB)]
            W.gT = sbw("gT", [128, 22, TB * 128], BF16)
            W.b_gT = [Buf() for _ in range(TB)]
            W.f32t = Rot([(sbw("f32t_%d" % i, [128, D], F32), Buf()) for i in range(2)])
            W.bf16t = Rot([(sbw("bf16t_%d" % i, [128, D], BF16), Buf()) for i in range(3)])
            W.half = Rot([(sbw("half_%d" % i, [128, 512], F32), Buf()) for i in range(4)])
            W.halfb = Rot([(sbw("halfb_%d" % i, [128, 512], BF16), Buf()) for i in range(3)])
            W.small = Rot([(sbw("small_%d" % i, [128, 8], F32), Buf()) for i in range(6)])
            W.biasb = Rot([(sbw("biasb_%d" % i, [128, 512], F32), Buf()) for i in range(2)])
            if phase_a:
                W.trT = Rot([(sbw("trT_%d" % i, [128, 4, 128], BF16), Buf()) for i in range(3)])
                W.tabs = Rot([(sbw("tabs_%d" % i, [128, 2, 256], F32), Buf()) for i in range(2)])
                W.krd_t = sbw("krd_t", [128, TB, D], BF16)
                W.b_krd = [Buf() for _ in range(TB)]
                W.vr_t = sbw("vr_t", [128, TB, D], BF16)
                W.b_vr = [Buf() for _ in range(TB)]
                W.Eacc = sbw("Eacc", [128, 4, 2, 256], F32)
                W.b_E = Buf()
                W.dec_t = sbw("dec_t", [128, TB, 8], F32)
                W.b_dec = [Buf() for _ in range(TB)]

        wcache = {}

        def load_w(Wd, KC, cols, rot, tag=None):
            wt, bw = rot.next()
            width = sum(wd for (_, wd, _) in cols)
            if not OPT_WC:
                tag = None
            if tag is not None and tag in wcache:
                scr, bscr = wcache[tag]
                k.dma("sp", wt[:, 0:KC, 0:width], scr.rearrange("p (c n) -> p c n", n=width), reads=[bscr], writes=[bw])
                return wt, bw
            for (c0, wd, d0) in cols:
                k.dma("pool", wt[:, 0:KC, d0:d0 + wd],
                      Wd[:, c0:c0 + wd].rearrange("(c p) n -> p c n", p=128), writes=[bw])
            if tag is not None:
                scr = dscr("wc_%s_%d" % (tag[0], tag[1]), [128, KC * width], BF16)
                bscr = Buf()
                k.dma("sp", scr.rearrange("p (c n) -> p c n", n=width), wt[:, 0:KC, 0:width], reads=[bw], writes=[bscr])
                wcache[tag] = (scr, bscr)
            return wt, bw

        def stream_w(loads, body):
            q = []
            n = len(loads)
            for i in range(min(2, n)):
                q.append(loads[i]())
            for i in range(n):
                if i + 2 < n:
                    q.append(loads[i + 2]())
                body(i, *q[i])

        def mm_block(lhs_fn, KC, wt, bw, width, nt, lbufs):
            pt, bp = pmm.next()
            for c in range(KC):
                k.op("pe", lambda e: e.matmul(pt[0:nt, 0:width], lhsT=lhs_fn(c), rhs=wt[:, c, 0:width],
                                              start=(c == 0), stop=(c == KC - 1)),
                     reads=list(lbufs) + [bw], writes=[bp])
            return pt, bp

        def transpose_to(src, bsrc, nt, ncol, dst_fn, bdst, f32=False):
            for c0 in range(0, ncol, 8):
                c1 = min(ncol, c0 + 8)
                pt, bp = ptr.next()
                for c in range(c0, c1):
                    k.op("pe", lambda e: e.transpose(out=pt[:, c - c0, 0:nt], in_=src[0:nt, c * 128:(c + 1) * 128],
                                                     identity=idb[0:nt, 0:nt]),
                         reads=[bsrc, b_id], writes=[bp])
                k.op("act", lambda e: e.copy(out=dst_fn(c0, c1), in_=pt[:, 0:c1 - c0, 0:nt]),
                     reads=[bp], writes=[bdst])

        def norm_mod(xt, bx, nt, gi, si, dst, bdst):
            sm, bsm = W.small.next()
            jt, bj = W.f32t.next()
            k.op("act", lambda e: e.activation(out=jt[0:nt, :], in_=xt, func=AF.Square, accum_out=sm[0:nt, 0:1]),
                 reads=[bx], writes=[bj, bsm])
            k.op("act", lambda e: e.activation(out=sm[0:nt, 1:2], in_=sm[0:nt, 0:1], func=AF.Sqrt, scale=1.0 / D, bias=eps_t[0:nt, 0:1]),
                 reads=[bsm, b_eps], writes=[bsm])
            k.op("dve", lambda e: e.reciprocal(out=sm[0:nt, 2:3], in_=sm[0:nt, 1:2]), reads=[bsm], writes=[bsm])
            k.op("dve", lambda e: e.scalar_tensor_tensor(out=jt[0:nt, :], in0=xt, scalar=sm[0:nt, 2:3],
                                                        in1=mod[0:nt, gi * D:(gi + 1) * D], op0=ALU.mult, op1=ALU.mult),
                 reads=[bx, bsm, b_mod], writes=[bj])
            k.op("pool", lambda e: e.tensor_tensor(out=dst, in0=jt[0:nt, :], in1=mod[0:nt, si * D:(si + 1) * D], op=ALU.add),
                 reads=[bj, b_mod], writes=[bdst])

        def compute_mod(row0, nt):
            ct, bc = W.f32t.next()
            k.dma("sp", ct[0:nt, :], ctok[row0:row0 + nt, :], writes=[bc])
            cb, bcb = W.bf16t.next()
            k.op("act", lambda e: e.activation(out=cb[0:nt, :], in_=ct[0:nt, :], func=AF.Silu), reads=[bc], writes=[bcb])
            cTt = sb_cT
            transpose_to(cb, bcb, nt, 8, lambda c0, c1: cTt[:, c0:c1, 0:nt], b_cT)
            def body(blk, wt, bw):
                bb, bbb = W.biasb.next()
                k.dma("sp", bb[0:nt, :], ada_b[:, blk * 512:(blk + 1) * 512].partition_broadcast(nt), writes=[bbb])
                pt, bp = mm_block(lambda c: cTt[:, c, 0:nt], 8, wt, bw, 512, nt, [b_cT])
                k.op("dve", lambda e: e.tensor_tensor(out=mod[0:nt, blk * 512:(blk + 1) * 512], in0=pt[0:nt, :], in1=bb[0:nt, :], op=ALU.add),
                     reads=[bp, bbb], writes=[b_mod])
            stream_w([(lambda blk=blk: load_w(ada_w, 8, [(blk * 512, 512, 0)], W.wk8, ("ada", blk))) for blk in range(18)], body)
            for (si, gi) in ((1, 0), (4, 1), (7, 2)):
                nt_, bn_ = W.f32t.next()
                k.dma("sp", nt_[0:nt, :], nrm[gi:gi + 1, :].partition_broadcast(nt), writes=[bn_])
                k.op("dve", lambda e: e.scalar_tensor_tensor(out=mod[0:nt, si * D:(si + 1) * D], in0=mod[0:nt, si * D:(si + 1) * D],
                                                            scalar=1.0, in1=nt_[0:nt, :], op0=ALU.add, op1=ALU.mult),
                     reads=[b_mod, bn_], writes=[b_mod])
            for gi in (2, 8):
                k.op("pool", lambda e: e.tensor_scalar(out=mod[0:nt, gi * D:(gi + 1) * D], in0=mod[0:nt, gi * D:(gi + 1) * D],
                                                       scalar1=0.5, scalar2=None, op0=ALU.mult),
                     reads=[b_mod], writes=[b_mod])

        sb_cT = sb("cT", [128, 8, 128], BF16)
        b_cT = Buf()

        def ffn(tiles, Win, Wout, sh_i, sc_i, gt_i, wtag):
            nT = len(tiles)
            for ti, (tok0, nt) in enumerate(tiles):
                hb, bhb = W.bf16t.next()
                norm_mod(W.xb[0:nt, ti, :], W.b_x[ti], nt, sc_i, sh_i, hb[0:nt, :], bhb)
                transpose_to(hb, bhb, nt, 8, lambda c0, c1: W.hT[:, c0:c1, ti * 128:ti * 128 + nt], W.b_hT[ti])
            def body_in(blk, wt, bw):
                for ti, (tok0, nt) in enumerate(tiles):
                    pt, bp = mm_block(lambda c: W.hT[:, c, ti * 128:ti * 128 + nt], 8, wt, bw, 512, nt, [W.b_hT[ti]])
                    st, bst = W.half.next()
                    k.op("act", lambda e: e.activation(out=st[0:nt, 0:256], in_=pt[0:nt, 0:256], func=AF.Silu), reads=[bp], writes=[bst])
                    k.op("dve", lambda e: e.tensor_tensor(out=W.gtok[0:nt, ti, blk * 256:(blk + 1) * 256], in0=pt[0:nt, 256:512],
                                                         in1=st[0:nt, 0:256], op=ALU.mult),
                         reads=[bp, bst], writes=[W.b_g[ti]])
            stream_w([(lambda blk=blk: load_w(Win, 8, [(blk * 256, 256, 0), (DFF + blk * 256, 256, 256)], W.wk8, (wtag + "i", blk))) for blk in range(11)], body_in)
            for ti, (tok0, nt) in enumerate(tiles):
                transpose_to(W.gtok[:, ti, :], W.b_g[ti], nt, 22, lambda c0, c1: W.gT[:, c0:c1, ti * 128:ti * 128 + nt], W.b_gT[ti])
            def body_out(blk, wt, bw):
                for ti, (tok0, nt) in enumerate(tiles):
                    pt, bp = mm_block(lambda c: W.gT[:, c, ti * 128:ti * 128 + nt], 22, wt, bw, 256, nt, [W.b_gT[ti]])
                    st, bst = W.half.next()
                    k.op("dve", lambda e: e.tensor_tensor(out=st[0:nt, 0:256], in0=pt[0:nt, 0:256],
                                                         in1=mod[0:nt, gt_i * D + blk * 256:gt_i * D + (blk + 1) * 256], op=ALU.mult),
                         reads=[bp, b_mod], writes=[bst])
                    k.op("pool", lambda e: e.tensor_tensor(out=W.xb[0:nt, ti, blk * 256:(blk + 1) * 256], in0=W.xb[0:nt, ti, blk * 256:(blk + 1) * 256],
                                                          in1=st[0:nt, 0:256], op=ALU.add),
                         reads=[bst, W.b_x[ti]], writes=[W.b_x[ti]])
            stream_w([(lambda blk=blk: load_w(Wout, 22, [(blk * 256, 256, 0)], W.wk22, (wtag + "o", blk))) for blk in range(4)], body_out)

        def rotate(pt, bp, nt, tabd, tok0, nh, pairmode, out32, bout):
            WID = 64 if not pairmode else 256
            tb, btb = W.tabs.next()
            k.dma("sp", tb[0:nt, :, 0:WID], tabd[tok0:tok0 + nt, :, :], writes=[btb])
            xc, bxc = W.half.next()
            k.op("act", lambda e: e.copy(out=xc[0:nt, :], in_=pt[0:nt, :]), reads=[bp], writes=[bxc])
            A, bA = W.half.next()
            B, bB = W.half.next()
            xv = xc[0:nt, :].rearrange("p (h w) -> p h w", w=WID)
            cb_ = tb[0:nt, 0, 0:WID].unsqueeze(1).broadcast_to([nt, nh, WID])
            sb_ = tb[0:nt, 1, 0:WID].unsqueeze(1).broadcast_to([nt, nh, WID])
            k.op("pool", lambda e: e.tensor_tensor(out=A[0:nt, :].rearrange("p (h w) -> p h w", w=WID), in0=xv, in1=cb_, op=ALU.mult),
                 reads=[bxc, btb], writes=[bA])
            k.op("dve", lambda e: e.tensor_tensor(out=B[0:nt, :].rearrange("p (h w) -> p h w", w=WID), in0=xv, in1=sb_, op=ALU.mult),
                 reads=[bxc, btb], writes=[bB])
            if not pairmode:
                def v(t, j):
                    return t[0:nt, :].rearrange("p (h two w) -> p h two w", two=2, w=32)[:, :, j, :]
            else:
                def v(t, j):
                    return t[0:nt, :].rearrange("p (h w two) -> p h w two", two=2, w=128)[:, :, :, j]
            k.op("dve", lambda e: e.tensor_tensor(out=v(out32, 0), in0=v(A, 0), in1=v(B, 1), op=ALU.subtract),
                 reads=[bA, bB], writes=[bout])
            k.op("pool", lambda e: e.tensor_tensor(out=v(out32, 1), in0=v(A, 1), in1=v(B, 0), op=ALU.add),
                 reads=[bA, bB, bout], writes=[bout])

        def tr4_to_dram(srcb, bsrc, nt, dram3, h0, tok0, bdram):
            tt, btt = W.trT.next()
            pt, bp = ptr.next()
            for c in range(4):
                k.op("pe", lambda e: e.transpose(out=pt[:, c, 0:nt], in_=srcb[0:nt, c * 128:(c + 1) * 128], identity=idb[0:nt, 0:nt]),
                     reads=[bsrc, b_id], writes=[bp])
            k.op("act", lambda e: e.copy(out=tt[:, :, 0:nt], in_=pt[:, 0:4, 0:nt]), reads=[bp], writes=[btt])
            k.dma("sp", dram3[h0:h0 + 4, :, tok0:tok0 + nt].rearrange("h p t -> p h t"), tt[:, :, 0:nt], reads=[btt], writes=[bdram])

        def mixer_in(tiles, is_sample):
            nT = len(tiles)
            for ti, (tok0, nt) in enumerate(tiles):
                hb, bhb = W.bf16t.next()
                norm_mod(W.xb[0:nt, ti, :], W.b_x[ti], nt, 4, 3, hb[0:nt, :], bhb)
                transpose_to(hb, bhb, nt, 8, lambda c0, c1: W.hT[:, c0:c1, ti * 128:ti * 128 + nt], W.b_hT[ti])
                k.dma("sp", W.dec_t[0:nt, ti, :], dec[tok0:tok0 + nt, :], writes=[W.b_dec[ti]])
                k.dma("sp", xmid_s[tok0:tok0 + nt, :], W.xb[0:nt, ti, :], reads=[W.b_x[ti]], writes=[bufs_d["xmid"]])
            def body_mix(blk, wt, bw):
                kind, sub = blk // 2, blk % 2
                if kind not in DBG_KINDS:
                    return
                for ti, (tok0, nt) in enumerate(tiles):
                    pt, bp = mm_block(lambda c: W.hT[:, c, ti * 128:ti * 128 + nt], 8, wt, bw, 512, nt, [W.b_hT[ti]])
                    cs = slice(sub * 512, (sub + 1) * 512)
                    if kind in (0, 1):
                        r32, br = W.half.next()
                        rotate(pt, bp, nt, ropeq if kind == 0 else ropek, tok0, 8, False, r32, br)
                        rb, brb = W.halfb.next()
                        k.op("act", lambda e: e.copy(out=rb[0:nt, :], in_=r32[0:nt, :]), reads=[br], writes=[brb])
                        if kind == 0:
                            tr4_to_dram(rb, brb, nt, qT_s, sub * 4, tok0, bufs_d["qT"])
                        else:
                            if is_sample:
                                k.dma("sp", k_s[tok0 - NPT:tok0 - NPT + nt, cs], r32[0:nt, :], reads=[br], writes=[bufs_d["out"]])
                                tr4_to_dram(rb, brb, nt, kTs_s, sub * 4, tok0 - NPT, bufs_d["kTs"])
                            else:
                                k.dma("sp", k_p[tok0:tok0 + nt, cs], r32[0:nt, :], reads=[br], writes=[bufs_d["out"]])
                                tr4_to_dram(rb, brb, nt, kT_x[tok0 // 512].rearrange("(h p) t -> h p t", p=128), sub * 4, tok0 % 512, bx_kT[tok0 // 512])
                    elif kind == 2:
                        r32, br = W.half.next()
                        k.op("act", lambda e: e.copy(out=r32[0:nt, :], in_=pt[0:nt, :]), reads=[bp], writes=[br])
                        rb, brb = W.halfb.next()
                        k.op("dve", lambda e: e.tensor_copy(out=rb[0:nt, :], in_=pt[0:nt, :]), reads=[bp], writes=[brb])
                        if is_sample:
                            k.dma("sp", v_s[tok0 - NPT:tok0 - NPT + nt, cs], r32[0:nt, :], reads=[br], writes=[bufs_d["out"]])
                            k.dma("sp", vs_s[tok0 - NPT:tok0 - NPT + nt, cs], rb[0:nt, :], reads=[brb], writes=[bufs_d["vs"]])
                        else:
                            k.dma("sp", v_p[tok0:tok0 + nt, cs], r32[0:nt, :], reads=[br], writes=[bufs_d["out"]])
                            k.dma("sp", v_x[tok0:tok0 + nt, cs], rb[0:nt, :], reads=[brb], writes=[bx_v[tok0 // 512]])
                    elif kind in (3, 4):
                        r32, br = W.half.next()
                        rotate(pt, bp, nt, retq if kind == 3 else retk, tok0, 2, True, r32, br)
                        rb, brb = W.halfb.next()
                        k.op("act", lambda e: e.copy(out=rb[0:nt, :], in_=r32[0:nt, :]), reads=[br], writes=[brb])
                        tr4_to_dram(rb, brb, nt, qrT_s if kind == 3 else krT_s, sub * 4, tok0, bufs_d["qrT" if kind == 3 else "krT"])
                        if kind == 4:
                            for hh in range(2):
                                h = sub * 2 + hh
                                k.op("dve", lambda e: e.tensor_scalar(out=W.krd_t[0:nt, ti, h * 256:(h + 1) * 256], in0=r32[0:nt, hh * 256:(hh + 1) * 256],
                                                                     scalar1=W.dec_t[0:nt, ti, h:h + 1], scalar2=None, op0=ALU.mult),
                                     reads=[br, W.b_dec[ti]], writes=[W.b_krd[ti]])
                            if sub == 1:
                                k.dma("sp", krd_s[tok0:tok0 + nt, :], W.krd_t[0:nt, ti, :], reads=[W.b_krd[ti]], writes=[bufs_d["krd"]])
                    elif kind == 5:
                        k.op("act", lambda e: e.copy(out=W.vr_t[0:nt, ti, cs], in_=pt[0:nt, :]), reads=[bp], writes=[W.b_vr[ti]])
                        if sub == 1:
                            k.dma("sp", vr_s[tok0:tok0 + nt, :], W.vr_t[0:nt, ti, :], reads=[W.b_vr[ti]], writes=[bufs_d["vr"]])
                    else:
                        r32, br = W.half.next()
                        k.op("act", lambda e: e.activation(out=r32[0:nt, :], in_=pt[0:nt, :], func=AF.Silu if kind == 6 else AF.Sigmoid),
                             reads=[bp], writes=[br])
                        gc = (kind - 6) * D + sub * 512
                        k.dma("sp", gate_s[tok0:tok0 + nt, gc:gc + 512], r32[0:nt, :], reads=[br], writes=[bufs_d["gate"]])
            stream_w([(lambda blk=blk: load_w(w_in, 8, [(blk * 512, 512, 0)], W.wk8, ("win", blk))) for blk in range(18)], body_mix)

        def group_state(u):
            for n in range(4):
                for h in range(4):
                    pt, bp = pmm.next()
                    for dt_ in range(2):
                        k.op("pe", lambda e: e.matmul(pt[:, dt_ * 256:(dt_ + 1) * 256],
                                                      lhsT=W.krd_t[:, n, h * 256 + dt_ * 128:h * 256 + (dt_ + 1) * 128],
                                                      rhs=W.vr_t[:, n, h * 256:(h + 1) * 256], start=True, stop=True),
                             reads=[W.b_krd[n], W.b_vr[n]], writes=[bp])
                    Ev = W.Eacc[:, h, :, :].rearrange("p a b -> p (a b)")
                    if n == 0:
                        k.op("act", lambda e: e.copy(out=Ev, in_=pt[:, :]), reads=[bp], writes=[W.b_E])
                    else:
                        k.op("dve", lambda e: e.scalar_tensor_tensor(out=Ev, in0=Ev, scalar=float(GAM[h] ** 128), in1=pt[:, :],
                                                                    op0=ALU.mult, op1=ALU.add),
                             reads=[bp, W.b_E], writes=[W.b_E])
            k.dma("sp", E_x[u], W.Eacc[:].rearrange("p h a b -> p (h a b)"), reads=[W.b_E], writes=[bx_E[u]])

        groups = [[0, 1, 2, 3], [4, 5, 6, 7]]
        bg = {n: [Buf() for _ in range(4)] for n in ("kT", "v", "E")}

        def exchange(u):
            k.coll(lambda e: e.collective_compute("AllGather", ALU.bypass, replica_groups=groups,
                                                  ins=[kT_x[u].opt()], outs=[kT_g[u].opt()]),
                   reads=[bx_kT[u]], writes=[bg["kT"][u]])
            k.coll(lambda e: e.collective_compute("AllGather", ALU.bypass, replica_groups=groups,
                                                  ins=[v_x[u * 512:(u + 1) * 512, :].opt()], outs=[v_g[u].opt()]),
                   reads=[bx_v[u]], writes=[bg["v"][u]])
            k.coll(lambda e: e.collective_compute("AllGather", ALU.bypass, replica_groups=groups,
                                                  ins=[E_x[u].opt()], outs=[E_g[u].opt()]),
                   reads=[bx_E[u]], writes=[bg["E"][u]])

        k.op("pool", lambda e: e.memset(eps_t[:], 1e-6), writes=[b_eps])
        k.op("pool", lambda e: e.memset(eps5_t[:], 1e-5), writes=[b_eps])
        k.op("pool", lambda e: e.memset(ones_b[:], 1.0), writes=[b_eps])
        esA = ExitStack()
        mk_work(esA, "a", True)
        compute_mod(0, 128)
        for u in range(4):
            tiles = [(u * 512 + i * 128, 128) for i in range(4)]
            for ti, (tok0, nt) in enumerate(tiles):
                k.dma("sp", W.xb[0:nt, ti, :], xp[tok0:tok0 + nt, :], writes=[W.b_x[ti]])
            ffn(tiles, f1_in, f1_out, 0, 1, 2, "f1")
            mixer_in(tiles, False)
            group_state(u)
            if OPT_PROG:
                exchange(u)
        compute_mod(128, 64)
        tiles = [(NPT, NST)]
        k.dma("sp", W.xb[0:NST, 0, :], xs[:, :], writes=[W.b_x[0]])
        ffn(tiles, f1_in, f1_out, 0, 1, 2, "f1")
        mixer_in(tiles, True)
        k.barrier()
        esA.close()
        if not OPT_PROG:
            for u in range(4):
                exchange(u)

        esC = ExitStack()
        cur = [es]

        def sbc(name, shape, dt):
            return cur[0].enter_context(nc.sbuf_tensor(name, list(shape), dt))

        def rot(name, shape, dt, n):
            return Rot([(sbc("%s_%d" % (name, i), shape, dt), Buf()) for i in range(n)])

        smallc = rot("smallc", [128, 8], F32, 8)
        lamv = sbc("lamv", [128, 4, 64], F32)
        b_lam = Buf()
        k.dma("sp", lamv[:], lam_d.partition_broadcast(128), writes=[b_lam])
        lam_t = sbc("lam_t", [128, 8], F32)
        lj = sbc("lj", [128, 64], F32)
        for i in range(2):
            k.op("dve", lambda e: e.tensor_tensor(out=lj[:], in0=lamv[:, 2 * i, :], in1=lamv[:, 2 * i + 1, :], op=ALU.mult),
                 reads=[b_lam], writes=[b_lam])
            k.op("dve", lambda e: e.tensor_reduce(out=lam_t[:, i:i + 1], in_=lj[:], axis=AX.X, op=ALU.add), reads=[b_lam], writes=[b_lam])
        k.op("act", lambda e: e.activation(out=lam_t[:, 2:4], in_=lam_t[:, 0:2], func=AF.Exp), reads=[b_lam], writes=[b_lam])
        k.op("dve", lambda e: e.tensor_tensor(out=lam_t[:, 4:5], in0=lam_t[:, 3:4], in1=lam_t[:, 2:3], op=ALU.subtract), reads=[b_lam], writes=[b_lam])
        k.op("dve", lambda e: e.tensor_scalar(out=lam_t[:, 4:5], in0=lam_t[:, 4:5], scalar1=-LAM_INIT, scalar2=None, op0=ALU.add), reads=[b_lam], writes=[b_lam])
        nlam = lam_t[:, 4:5]
        subg = sbc("subg", [128, 128], F32)
        b_subg = Buf()
        k.dma("sp", subg[:], subln_d.partition_broadcast(128), writes=[b_subg])
        k.op("dve", lambda e: e.tensor_scalar(out=subg[:], in0=subg[:], scalar1=1.0 - LAM_INIT, scalar2=None, op0=ALU.mult), reads=[b_subg], writes=[b_subg])
        rng = sbc("rng", [128, D], F32)
        b_rng = Buf()
        k.dma("sp", rng[:], nrm[4:5, :].partition_broadcast(128), writes=[b_rng])

        def subln(attn, battn, nt, dst, bdst):
            sm, bsm = smallc.next()
            jt, bj = junk.next()
            k.op("act", lambda e: e.activation(out=jt[0:nt, 0:128], in_=attn, func=AF.Square, accum_out=sm[0:nt, 0:1]),
                 reads=[battn], writes=[bj, bsm])
            k.op("act", lambda e: e.activation(out=sm[0:nt, 1:2], in_=sm[0:nt, 0:1], func=AF.Sqrt, scale=1.0 / 128, bias=eps5_t[0:nt, 0:1]),
                 reads=[bsm, b_eps], writes=[bsm])
            k.op("dve", lambda e: e.reciprocal(out=sm[0:nt, 2:3], in_=sm[0:nt, 1:2]), reads=[bsm], writes=[bsm])
            k.op("dve", lambda e: e.scalar_tensor_tensor(out=dst, in0=attn, scalar=sm[0:nt, 2:3], in1=subg[0:nt, :], op0=ALU.mult, op1=ALU.mult),
                 reads=[battn, bsm, b_subg], writes=[bdst])

        junk = rot("junk", [128, 256], F32, 2)
        cur[0] = esC

        if DO_ATTN:
            KT = sbc("KT", [128, 4, 4, 512], BF16)
            b_KT = Buf()
            Vt = sbc("Vt", [128, 4, 4, 4, 128], BF16)
            b_Vt = Buf()
            QT = sbc("QT", [128, NPT], BF16)
            b_QT = Buf()
            maskt = sbc("maskt", [128, 16, 512], BF16)
            b_mk = Buf()
            for r4 in range(4):
                k.dma("pool", maskt[:, r4 * 4:(r4 + 1) * 4, :], maskd[:, r4 * 4:(r4 + 1) * 4, :], writes=[b_mk])
            PT = rot("PT", [128, 512], BF16, 3)
            Osb = [(sbc("Osb%d" % c, [128, 512], F32), Buf()) for c in range(2)]
            lsb = sbc("lsb", [1, 2, 512], F32)
            b_lsb = Buf()
            t12 = rot("t12", [128, 128], F32, 4)
            oat = rot("oat", [128, 128], F32, 3)
            Sbank = [pmm.items[0], pmm.items[1]]
            Obank = [pmm.items[2], pmm.items[3]]
            sidx = 0
            for h in range(8):
                k.dma("sp", QT[:, :], qT_s[h, :, 0:NPT], reads=[bufs_d["qT"]], writes=[b_QT])
                for jj in range(4):
                    for u in range(4):
                        r0 = jj * 1024 + h * 128
                        k.dma("sp", KT[:, u, jj, :], kT_g[u][r0:r0 + 128, :], reads=[bg["kT"][u]], writes=[b_KT])
                        k.dma("sp", Vt[:, u, jj, :, :],
                              v_g[u][jj * 512:(jj + 1) * 512, h * 128:(h + 1) * 128].rearrange("(kb p) d -> p kb d", p=128),
                              reads=[bg["v"][u]], writes=[b_Vt])
                for u in range(4):
                    nB = 16 * u + 16
                    its = [(B, c) for B in range(nB) for c in range(2)]

                    def emit_qk(n):
                        B, c = its[n]
                        G, kb = B // 4, B % 4
                        gu, gj = G // 4, G % 4
                        pss, bss = Sbank[n % 2]
                        k.op("pe", lambda e: e.matmul(pss[:, :], lhsT=KT[c * 64:(c + 1) * 64, gu, gj, kb * 128:(kb + 1) * 128],
                                                      rhs=QT[c * 64:(c + 1) * 64, u * 512:(u + 1) * 512], start=True, stop=True),
                             reads=[b_KT, b_QT], writes=[bss])

                    emit_qk(0)
                    for n, (B, c) in enumerate(its):
                        if OPT_PIPE and n + 1 < len(its):
                            emit_qk(n + 1)
                        if not OPT_PIPE and n > 0:
                            emit_qk(n)
                        G, kb = B // 4, B % 4
                        gu, gj = G // 4, G % 4
                        pss, bss = Sbank[n % 2]
                        pt_, bpt = PT.next()
                        k.op("act", lambda e: e.activation(out=pt_[:, :], in_=pss[:, :], func=AF.Exp), reads=[bss], writes=[bpt])
                        if B >= 16 * u:
                            k.op("pool", lambda e: e.tensor_tensor(out=pt_[:, :], in0=pt_[:, :], in1=maskt[:, B - 16 * u, :], op=ALU.mult),
                                 reads=[b_mk], writes=[bpt])
                        k.op("pe", lambda e: e.matmul(Obank[c][0][:, :], lhsT=Vt[:, gu, gj, kb, :], rhs=pt_[:, :],
                                                      start=(B == 0), stop=(B == nB - 1)),
                             reads=[b_Vt, bpt], writes=[Obank[c][1]])
                        k.op("pe", lambda e: e.matmul(pl[c][0][0:1, :], lhsT=ones_b[:, 0:1], rhs=pt_[:, :],
                                                      start=(B == 0), stop=(B == nB - 1)),
                             reads=[bpt, b_eps], writes=[pl[c][1]])
                    for c in range(2):
                        k.op("act" if c == 0 else "dve",
                             (lambda e: e.copy(out=Osb[c][0][:, :], in_=Obank[c][0][:, :])) if c == 0 else
                             (lambda e: e.tensor_copy(out=Osb[c][0][:, :], in_=Obank[c][0][:, :])),
                             reads=[Obank[c][1]], writes=[Osb[c][1]])
                        k.op("dve", lambda e: e.tensor_copy(out=lsb[0:1, c, :], in_=pl[c][0][0:1, :]), reads=[pl[c][1]], writes=[b_lsb])
                    for qt in range(4):
                        pss, bss = Sbank[sidx]
                        sidx ^= 1
                        qs = slice(qt * 128, (qt + 1) * 128)
                        for c in range(2):
                            k.op("pe", lambda e: e.transpose(out=pss[:, c * 128:(c + 1) * 128], in_=Osb[c][0][:, qs], identity=idf[:, :]),
                                 reads=[Osb[c][1], b_id], writes=[bss])
                            k.op("pe", lambda e: e.transpose(out=pss[:, 256 + c:257 + c], in_=lsb[0:1, c, qs], identity=idf[0:1, 0:1]),
                                 reads=[b_lsb, b_id], writes=[bss])
                        sm, bsm = smallc.next()
                        k.op("dve", lambda e: e.reciprocal(out=sm[:, 0:2], in_=pss[:, 256:258]), reads=[bss], writes=[bsm])
                        t1, bt1 = t12.next()
                        t2, bt2 = t12.next()
                        k.op("dve", lambda e: e.tensor_scalar(out=t1[:, :], in0=pss[:, 0:128], scalar1=sm[:, 0:1], scalar2=None, op0=ALU.mult),
                             reads=[bss, bsm], writes=[bt1])
                        k.op("dve", lambda e: e.tensor_scalar(out=t2[:, :], in0=pss[:, 128:256], scalar1=sm[:, 1:2], scalar2=None, op0=ALU.mult),
                             reads=[bss, bsm], writes=[bt2])
                        k.op("dve", lambda e: e.scalar_tensor_tensor(out=t1[:, :], in0=t2[:, :], scalar=nlam, in1=t1[:, :], op0=ALU.mult, op1=ALU.add),
                             reads=[bt2, b_lam], writes=[bt1])
                        ot, bot = oat.next()
                        subln(t1[:, :], bt1, 128, ot[:, :], bot)
                        tok0 = u * 512 + qt * 128
                        k.dma("sp", oa_s[tok0:tok0 + 128, h * 128:(h + 1) * 128], ot[:, :], reads=[bot], writes=[bufs_d["oa"]])

        k.barrier()
        esC.close()
        esC = ExitStack()
        cur[0] = esC
        if DO_RET:
            Sown = sbc("Sown", [128, 4, 2048], F32)
            b_Sown = [Buf() for _ in range(4)]
            R = sbc("R", [128, 2048], F32)
            b_R = Buf()
            Eld = rot("Eld", [128, 2048], F32, 2)
            mskt = sbc("mskt", [128, 4], F32)
            b_msk = Buf()
            k.dma("sp", mskt[:], msk_d, writes=[b_msk])
            k.op("pool", lambda e: e.memset(R[:], 0.0), writes=[b_R])
            for G in range(16):
                u, jj = G // 4, G % 4
                if jj == 0:
                    k.op("dve", lambda e: e.tensor_scalar(out=Sown[:, u, :], in0=R[:, :], scalar1=mskt[:, 0:1], scalar2=None, op0=ALU.mult),
                         reads=[b_R, b_msk], writes=[b_Sown[u]])
                else:
                    k.op("dve", lambda e: e.scalar_tensor_tensor(out=Sown[:, u, :], in0=R[:, :], scalar=mskt[:, jj:jj + 1], in1=Sown[:, u, :],
                                                                op0=ALU.mult, op1=ALU.add),
                         reads=[b_R, b_msk, b_Sown[u]], writes=[b_Sown[u]])
                et, bet = Eld.next()
                k.dma("sp", et[:, :], E_g[u][jj * 128:(jj + 1) * 128, :], reads=[bg["E"][u]], writes=[bet])
                for h in range(4):
                    hs = slice(h * 512, (h + 1) * 512)
                    k.op("dve" if h % 2 == 0 else "pool",
                         (lambda e: e.scalar_tensor_tensor(out=R[:, hs], in0=R[:, hs], scalar=float(GAM[h] ** 512), in1=et[:, hs], op0=ALU.mult, op1=ALU.add))
                         if h % 2 == 0 else
                         (lambda e: e.tensor_scalar(out=R[:, hs], in0=R[:, hs], scalar1=float(GAM[h] ** 512), scalar2=None, op0=ALU.mult)),
                         reads=[b_R, bet], writes=[b_R])
                    if h % 2 == 1:
                        k.op("pool", lambda e: e.tensor_tensor(out=R[:, hs], in0=R[:, hs], in1=et[:, hs], op=ALU.add), reads=[b_R, bet], writes=[b_R])
            k.dma("sp", st_p.rearrange("h (a p) b -> p h a b", p=128), R[:, :].rearrange("p (h a b) -> p h a b", h=4, a=2),
                  reads=[b_R], writes=[bufs_d["out"]])

            S = sbc("S", [128, 2048], F32)
            b_S = Buf()
            Sb = sbc("Sb", [128, 2048], BF16)
            b_Sb = Buf()
            DM = sbc("DM", [128, 4, 128], F32)
            b_DM = Buf()
            qrc = rot("qrc", [128, 8, 128], BF16, 2)
            krc = rot("krc", [128, 8, 128], BF16, 2)
            krdc = rot("krdc", [128, D], BF16, 2)
            vrc = rot("vrc", [128, D], BF16, 2)
            grt = rot("grt", [128, D], F32, 2)
            ort = rot("ort", [128, D], F32, 2)
            qdc = rot("qdc", [128, 8], F32, 2)
            smr = rot("smr", [128, 128], BF16, 2)
            inn = rot("inn", [128, 256], F32, 2)
            bst = rot("bst", [128, 8], F32, 4)

            def ret_tile(tok0, nt, DMd, cross_fn, upd_fn):
                q_, bq = qrc.next()
                k_, bk = krc.next()
                kd, bkd = krdc.next()
                v_, bv = vrc.next()
                g_, bgr = grt.next()
                o_, bo = ort.next()
                qd, bqd = qdc.next()
                k.dma("sp", q_[:, :, 0:nt], qrT_s[:, :, tok0:tok0 + nt].rearrange("h p t -> p h t"), reads=[bufs_d["qrT"]], writes=[bq])
                k.dma("sp", k_[:, :, 0:nt], krT_s[:, :, tok0:tok0 + nt].rearrange("h p t -> p h t"), reads=[bufs_d["krT"]], writes=[bk])
                k.dma("sp", kd[0:nt, :], krd_s[tok0:tok0 + nt, :], reads=[bufs_d["krd"]], writes=[bkd])
                k.dma("sp", v_[0:nt, :], vr_s[tok0:tok0 + nt, :], reads=[bufs_d["vr"]], writes=[bv])
                k.dma("sp", g_[0:nt, :], gate_s[tok0:tok0 + nt, 0:D], reads=[bufs_d["gate"]], writes=[bgr])
                k.dma("sp", qd[0:nt, :], dec[tok0:tok0 + nt, :], writes=[bqd])
                k.dma("sp", DM[0:nt, :, 0:nt], DMd, writes=[b_DM])
                ops = (q_, bq, k_, bk, kd, bkd, v_, bv)
                crosses = cross_fn(ops)
                for h in range(4):
                    ps_, bps = pl[0]
                    for dt_ in range(2):
                        k.op("pe", lambda e: e.matmul(ps_[0:nt, 0:nt], lhsT=k_[:, h * 2 + dt_, 0:nt], rhs=q_[:, h * 2 + dt_, 0:nt],
                                                      start=(dt_ == 0), stop=(dt_ == 1)),
                             reads=[bk, bq], writes=[bps])
                    sm_, bsm_ = smr.next()
                    k.op("dve", lambda e: e.tensor_tensor(out=sm_[0:nt, 0:nt], in0=ps_[0:nt, 0:nt], in1=DM[0:nt, h, 0:nt], op=ALU.mult),
                         reads=[bps, b_DM], writes=[bsm_])
                    pi_, bpi = pl[1]
                    k.op("pe", lambda e: e.matmul(pi_[0:nt, 0:256], lhsT=sm_[0:nt, 0:nt], rhs=v_[0:nt, h * 256:(h + 1) * 256], start=True, stop=True),
                         reads=[bsm_, bv], writes=[bpi])
                    in_, bin_ = inn.next()
                    k.op("act", lambda e: e.copy(out=in_[0:nt, :], in_=pi_[0:nt, 0:256]), reads=[bpi], writes=[bin_])
                    pc_, bpc = crosses[h]
                    k.op("dve", lambda e: e.scalar_tensor_tensor(out=o_[0:nt, h * 256:(h + 1) * 256], in0=pc_[0:nt, 0:256], scalar=qd[0:nt, 4 + h:5 + h],
                                                                in1=in_[0:nt, :], op0=ALU.mult, op1=ALU.add),
                         reads=[bpc, bqd, bin_], writes=[bo])
                    if upd_fn is not None:
                        upd_fn(h, ops)
                    st_, bst_ = bst.next()
                    mv, bmv = bst.next()
                    osl = o_[0:nt, h * 256:(h + 1) * 256]
                    k.op("dve", lambda e: e.bn_stats(out=st_[0:nt, 0:6], in_=osl), reads=[bo], writes=[bst_])
                    k.op("dve", lambda e: e.bn_aggr(out=mv[0:nt, 0:2], in_=st_[0:nt, 0:6]), reads=[bst_], writes=[bmv])
                    k.op("act", lambda e: e.activation(out=mv[0:nt, 2:3], in_=mv[0:nt, 1:2], func=AF.Sqrt, bias=eps5_t[0:nt, 0:1]),
                         reads=[bmv, b_eps], writes=[bmv])
                    k.op("dve", lambda e: e.reciprocal(out=mv[0:nt, 3:4], in_=mv[0:nt, 2:3]), reads=[bmv], writes=[bmv])
                    k.op("dve", lambda e: e.scalar_tensor_tensor(out=mv[0:nt, 4:5], in0=mv[0:nt, 0:1], scalar=-1.0, in1=mv[0:nt, 3:4],
                                                                op0=ALU.mult, op1=ALU.mult),
                         reads=[bmv], writes=[bmv])
                    k.op("act", lambda e: e.activation(out=osl, in_=osl, func=AF.Identity, scale=mv[0:nt, 3:4], bias=mv[0:nt, 4:5]),
                         reads=[bmv, bo], writes=[bo])
                    k.op("pool", lambda e: e.tensor_tensor(out=osl, in0=osl, in1=rng[0:nt, h * 256:(h + 1) * 256], op=ALU.mult),
                         reads=[bo, b_rng], writes=[bo])
                    k.op("pool", lambda e: e.tensor_tensor(out=osl, in0=osl, in1=g_[0:nt, h * 256:(h + 1) * 256], op=ALU.mult),
                         reads=[bo, bgr], writes=[bo])
                k.dma("sp", or_s[tok0:tok0 + nt, :], o_[0:nt, :], reads=[bo], writes=[bufs_d["or"]])

            for u in range(4):
                k.op("act", lambda e: e.copy(out=S[:, :], in_=Sown[:, u, :]), reads=[b_Sown[u]], writes=[b_S])
                for n in range(4):
                    tok0 = u * 512 + n * 128
                    k.op("act", lambda e: e.copy(out=Sb[:, :], in_=S[:, :]), reads=[b_S], writes=[b_Sb])

                    def cross_fn(ops):
                        q_, bq = ops[0], ops[1]
                        res = []
                        for h in range(4):
                            pc_, bpc = pmm.items[h]
                            for dt_ in range(2):
                                k.op("pe", lambda e: e.matmul(pc_[:, 0:256], lhsT=q_[:, h * 2 + dt_, :], rhs=Sb[:, (h * 2 + dt_) * 256:(h * 2 + dt_ + 1) * 256],
                                                              start=(dt_ == 0), stop=(dt_ == 1)),
                                     reads=[bq, b_Sb], writes=[bpc])
                            res.append((pc_, bpc))
                        return res

                    def upd_fn(h, ops):
                        kd, bkd, v_, bv = ops[4], ops[5], ops[6], ops[7]
                        pd, bpd = pmm.items[h]
                        for dt_ in range(2):
                            k.op("pe", lambda e: e.matmul(pd[:, dt_ * 256:(dt_ + 1) * 256], lhsT=kd[:, h * 256 + dt_ * 128:h * 256 + (dt_ + 1) * 128],
                                                          rhs=v_[:, h * 256:(h + 1) * 256], start=True, stop=True),
                                 reads=[bkd, bv], writes=[bpd])
                        hs = slice(h * 512, (h + 1) * 512)
                        k.op("dve", lambda e: e.scalar_tensor_tensor(out=S[:, hs], in0=S[:, hs], scalar=float(GAM[h] ** 128), in1=pd[:, :],
                                                                    op0=ALU.mult, op1=ALU.add),
                             reads=[bpd, b_S], writes=[b_S])

                    ret_tile(tok0, 128, dmp_d, cross_fn, upd_fn)

            qz = sbc("qz", [128, 16, 8, 64], BF16)
            b_qz = Buf()
            k.op("pool", lambda e: e.memset(qz[:], 0.0), writes=[b_qz])
            seqm = sbc("seqm_t", [64, 16], F32)
            b_seqm = Buf()
            k.dma("sp", seqm[:], seqm_d, writes=[b_seqm])
            Stt = rot("Stt", [128, 2048], F32, 2)
            Snew = rot("Snew", [128, 2048], F32, 2)
            kdz = rot("kdz", [64, D], BF16, 2)

            def cross_s(ops):
                q_, bq, kd, bkd, v_, bv = ops[0], ops[1], ops[4], ops[5], ops[6], ops[7]
                for i in range(16):
                    k.op("dve", lambda e: e.tensor_copy(out=qz[:, i, :, i * 4:(i + 1) * 4], in_=q_[:, :, i * 4:(i + 1) * 4]), reads=[bq], writes=[b_qz])
                for i in range(16):
                    st_, bst_ = Stt.next()
                    k.dma("sp", st_[:, :].rearrange("p (h a b) -> p h a b", h=4, a=2), st_in[i].rearrange("h (a p) b -> p h a b", p=128), writes=[bst_])
                    k.op("act", lambda e: e.copy(out=Sb[:, :], in_=st_[:, :]), reads=[bst_], writes=[b_Sb])
                    for h in range(4):
                        pc_, bpc = pmm.items[h]
                        for dt_ in range(2):
                            k.op("pe", lambda e: e.matmul(pc_[0:64, 0:256], lhsT=qz[:, i, h * 2 + dt_, :], rhs=Sb[:, (h * 2 + dt_) * 256:(h * 2 + dt_ + 1) * 256],
                                                          start=(i == 0 and dt_ == 0), stop=(i == 15 and dt_ == 1)),
                                 reads=[b_qz, b_Sb], writes=[bpc])
                    kz, bkz = kdz.next()
                    k.op("dve", lambda e: e.tensor_scalar(out=kz[0:64, :], in0=kd[0:64, :], scalar1=seqm[0:64, i:i + 1], scalar2=None, op0=ALU.mult),
                         reads=[bkd, b_seqm], writes=[bkz])
                    sn, bsn = Snew.next()
                    for h in range(4):
                        pd, bpd = pl[h % 2]
                        for dt_ in range(2):
                            k.op("pe", lambda e: e.matmul(pd[:, dt_ * 256:(dt_ + 1) * 256], lhsT=kz[0:64, h * 256 + dt_ * 128:h * 256 + (dt_ + 1) * 128],
                                                          rhs=v_[0:64, h * 256:(h + 1) * 256], start=True, stop=True),
                                 reads=[bkz, bv], writes=[bpd])
                        hs = slice(h * 512, (h + 1) * 512)
                        k.op("dve", lambda e: e.scalar_tensor_tensor(out=sn[:, hs], in0=st_[:, hs], scalar=float(GAM[h] ** 4), in1=pd[:, :],
                                                                    op0=ALU.mult, op1=ALU.add),
                             reads=[bpd, bst_], writes=[bsn])
                    k.dma("sp", st_s[i].rearrange("h (a p) b -> p h a b", p=128), sn[:, :].rearrange("p (h a b) -> p h a b", h=4, a=2),
                          reads=[bsn], writes=[bufs_d["out"]])
                return [pmm.items[h] for h in range(4)]

            ret_tile(NPT, NST, dms_d, cross_s, None)
        k.barrier()
        esC.close()

        esS = ExitStack()
        cur[0] = esS
        if DO_SATT:
            ptt = sbc("ptt", [128, 256], I32)
            iot = sbc("iot", [128, 1], I32)
            idx = sbc("idx", [128, 256], I32)
            b_idx = Buf()
            k.dma("sp", ptt[:], pt_d.partition_broadcast(128), writes=[b_idx])
            k.op("pool", lambda e: e.iota(iot[:], pattern=[[0, 1]], base=0, channel_multiplier=1), writes=[b_idx])
            k.op("dve", lambda e: e.tensor_scalar(out=idx[:], in0=ptt[:], scalar1=128, scalar2=iot[:, 0:1], op0=ALU.mult, op1=ALU.add),
                 reads=[b_idx], writes=[b_idx])
            qTs = sbc("qTs", [128, 8, 64], BF16)
            kTn = sbc("kTn", [128, 8, 64], BF16)
            b_qk = Buf()
            k.dma("sp", qTs[:], qT_s[:, :, NPT:NTOK].rearrange("h p t -> p h t"), reads=[bufs_d["qT"]], writes=[b_qk])
            k.dma("sp", kTn[:], kTs_s.rearrange("h p t -> p h t"), reads=[bufs_d["kTs"]], writes=[b_qk])
            Qblk = sbc("Qblk", [128, 8, 16, 8], BF16)
            k.op("pool", lambda e: e.memset(Qblk[:], 0.0), writes=[b_qk])
            for c in range(2):
                k.op("dve", lambda e: e.tensor_copy(out=Qblk[c * 64:(c + 1) * 64, :, :, c * 4:(c + 1) * 4],
                                                    in_=qTs[c * 64:(c + 1) * 64, :, :].rearrange("p h (i q) -> p h i q", q=4)),
                     reads=[b_qk], writes=[b_qk])
            satab = sbc("satab_t", [64, 8 * 128 + 8], F32)
            b_sat = Buf()
            k.dma("sp", satab[:], satab_d, writes=[b_sat])
            tri = sbc("tri_t", [4, 64], F32)
            k.dma("sp", tri[:], tri_d, writes=[b_sat])
            csel = sbc("csel", [64, 1], F32)
            k.op("dve", lambda e: e.scalar_tensor_tensor(out=csel[:], in0=satab[:, 1025:1026], scalar=nlam[0:64, :], in1=satab[:, 1024:1025],
                                                        op0=ALU.mult, op1=ALU.add), reads=[b_sat, b_lam], writes=[b_sat])
            pg32 = rot("pg32", [128, D], F32, 4)
            pgb = rot("pgb", [128, D], BF16, 3)
            KTp = rot("KTp", [128, 8, 128], BF16, 2)
            PTs = rot("PTs", [128, 64], BF16, 3)
            vnew = rot("vnew", [4, D], BF16, 2)
            msk_sb = sbc("msk_sb", [64, D], F32)
            b_msb = Buf()
            coef = rot("coef", [64, 2], F32, 2)
            o4 = rot("o4", [4, D], F32, 2)
            accb = [pmm.items[2], pmm.items[3]]
            outb = [pmm.items[0], pmm.items[1]]
            ps_sc, b_sc = pl[0]
            ps_l, b_l = pl[1]
            flip = 0
            for i in range(16 if DBG_SA >= 1 else 0):
                for pg in range(17):
                    last = (pg == 16)
                    if not last:
                        n = i * 16 + pg
                        kp, bkp = pg32.next()
                        k.dma("pool", None, None, reads=[b_idx], writes=[bkp],
                              fn=lambda e: e.indirect_dma_start(out=kp[:, :], out_offset=None, in_=cache_k2,
                                                                in_offset=bass.IndirectOffsetOnAxis(ap=idx[:, n:n + 1], axis=0)))
                        vp, bvp = pg32.next()
                        k.dma("pool", None, None, reads=[b_idx], writes=[bvp],
                              fn=lambda e: e.indirect_dma_start(out=vp[:, :], out_offset=None, in_=cache_v2,
                                                                in_offset=bass.IndirectOffsetOnAxis(ap=idx[:, n:n + 1], axis=0)))
                        kb_, bkb = pgb.next()
                        k.op("dve", lambda e: e.tensor_copy(out=kb_[:, :], in_=kp[:, :]), reads=[bkp], writes=[bkb])
                        ptT, bptT = ptr.next()
                        for c in range(8):
                            k.op("pe", lambda e: e.transpose(out=ptT[:, c, :], in_=kb_[:, c * 128:(c + 1) * 128], identity=idb[:, :]),
                                 reads=[bkb, b_id], writes=[bptT])
                        kt_, bkt = KTp.next()
                        k.op("act", lambda e: e.copy(out=kt_[:, :, :], in_=ptT[:, :, :]), reads=[bptT], writes=[bkt])
                        vb_, bvb = pgb.next()
                        k.op("act" if flip else "dve",
                             (lambda e: e.copy(out=vb_[:, :], in_=vp[:, :])) if flip else (lambda e: e.tensor_copy(out=vb_[:, :], in_=vp[:, :])),
                             reads=[bvp], writes=[bvb])
                        flip ^= 1
                        nk = 128
                        for h in range(8 if (DBG_SA >= 2 and DBG_SV != 2) else 0):
                            k.op("pe", lambda e: e.matmul(ps_sc[0:128, h * 8:(h + 1) * 8], lhsT=kt_[:, h, :], rhs=Qblk[:, h, i, :], start=True, stop=True),
                                 reads=[bkt, b_qk], writes=[b_sc])
                        vrows = vb_
                    else:
                        nk = 4
                        for h in range(8 if (DBG_SA >= 2 and DBG_SV != 1) else 0):
                            k.op("pe", lambda e: e.matmul(ps_sc[0:4, h * 8:(h + 1) * 8], lhsT=kTn[:, h, i * 4:(i + 1) * 4], rhs=Qblk[:, h, i, :], start=True, stop=True),
                                 reads=[b_qk], writes=[b_sc])
                        vrows, bvb = vnew.next()
                        k.dma("sp", vrows[0:4, :], vs_s[i * 4:(i + 1) * 4, :], reads=[bufs_d["vs"]], writes=[bvb])
                    if DBG_SA < 3:
                        continue
                    p_, bp_ = PTs.next()
                    k.op("act", lambda e: e.activation(out=p_[0:nk, :], in_=ps_sc[0:nk, 0:64], func=AF.Exp), reads=[b_sc], writes=[bp_])
                    if last:
                        k.op("dve", lambda e: e.tensor_tensor(out=p_[0:4, :], in0=p_[0:4, :], in1=tri[0:4, :], op=ALU.mult), reads=[b_sat], writes=[bp_])
                    if DBG_SA < 4:
                        continue
                    for hf in range(2):
                        k.op("pe", lambda e: e.matmul(accb[hf][0][0:64, :], lhsT=p_[0:nk, 0:64], rhs=vrows[0:nk, hf * 512:(hf + 1) * 512],
                                                      start=(pg == 0), stop=last),
                             reads=[bp_, bvb], writes=[accb[hf][1]])
                    k.op("pe", lambda e: e.matmul(ps_l[0:64, 0:1], lhsT=p_[0:nk, 0:64], rhs=ones_b[0:nk, 0:1], start=(pg == 0), stop=last),
                         reads=[bp_, b_eps], writes=[b_l])
                if DBG_SA < 5:
                    continue
                cf, bcf = coef.next()
                k.op("dve", lambda e: e.reciprocal(out=cf[:, 0:1], in_=ps_l[0:64, 0:1]), reads=[b_l], writes=[bcf])
                k.op("dve", lambda e: e.tensor_tensor(out=cf[:, 1:2], in0=cf[:, 0:1], in1=csel[:, 0:1], op=ALU.mult), reads=[b_sat], writes=[bcf])
                for hf in range(2):
                    k.op("dve", lambda e: e.scalar_tensor_tensor(out=msk_sb[:, hf * 512:(hf + 1) * 512], in0=accb[hf][0][0:64, :], scalar=cf[:, 1:2],
                                                                in1=satab[:, hf * 512:(hf + 1) * 512], op0=ALU.mult, op1=ALU.mult),
                         reads=[accb[hf][1], bcf, b_sat], writes=[b_msb])
                o_, bo_ = o4.next()
                for hf in range(2):
                    k.op("pe", lambda e: e.matmul(outb[hf][0][0:4, :], lhsT=satab[:, 1026:1030], rhs=msk_sb[:, hf * 512:(hf + 1) * 512], start=True, stop=True),
                         reads=[b_msb, b_sat], writes=[outb[hf][1]])
                    k.op("act", lambda e: e.copy(out=o_[0:4, hf * 512:(hf + 1) * 512], in_=outb[hf][0][0:4, :]), reads=[outb[hf][1]], writes=[bo_])
                k.dma("sp", attn_s[i * 4:(i + 1) * 4, :], o_[0:4, :], reads=[bo_], writes=[bufs_d["attn"]])
            at_ = sbc("at_", [64, D], F32)
            b_at = Buf()
            k.dma("sp", at_[:, :], attn_s[:, :], reads=[bufs_d["attn"]], writes=[b_at])
            ao_ = sbc("ao_", [64, D], F32)
            b_ao = Buf()
            for h in range(8):
                subln(at_[0:64, h * 128:(h + 1) * 128], b_at, 64, ao_[0:64, h * 128:(h + 1) * 128], b_ao)
            k.dma("sp", oa_s[NPT:NTOK, :], ao_[:, :], reads=[b_ao], writes=[bufs_d["oa"]])
        k.barrier()
        esS.close()

        esD = ExitStack()
        mk_work(esD, "d", False)
        oat_d = Rot([(esD.enter_context(nc.sbuf_tensor("oat_d%d" % i, [128, D], F32)), Buf()) for i in range(1)])
        ort_d = Rot([(esD.enter_context(nc.sbuf_tensor("ort_d%d" % i, [128, D], F32)), Buf()) for i in range(1)])
        gat_d = Rot([(esD.enter_context(nc.sbuf_tensor("gat_d%d" % i, [128, 2 * D], F32)), Buf()) for i in range(1)])
        nf_t = esD.enter_context(nc.sbuf_tensor("nf_t", [128, D], F32))
        b_nf = Buf()
        k.dma("sp", nf_t[:], nrm[3:4, :].partition_broadcast(128), writes=[b_nf])

        def phase_d(tiles, y_out, yoff):
            for ti, (tok0, nt) in enumerate(tiles):
                k.dma("sp", W.xb[0:nt, ti, :], xmid_s[tok0:tok0 + nt, :], reads=[bufs_d["xmid"]], writes=[W.b_x[ti]])
                oa_, boa = oat_d.next()
                or_, bor = ort_d.next()
                ga_, bga = gat_d.next()
                k.dma("sp", oa_[0:nt, :], oa_s[tok0:tok0 + nt, :], reads=[bufs_d["oa"]], writes=[boa])
                k.dma("sp", or_[0:nt, :], or_s[tok0:tok0 + nt, :], reads=[bufs_d["or"]], writes=[bor])
                k.dma("sp", ga_[0:nt, :], gate_s[tok0:tok0 + nt, D:3 * D], reads=[bufs_d["gate"]], writes=[bga])
                k.op("dve", lambda e: e.tensor_tensor(out=oa_[0:nt, :], in0=oa_[0:nt, :], in1=ga_[0:nt, 0:D], op=ALU.mult), reads=[bga], writes=[boa])
                k.op("pool", lambda e: e.tensor_tensor(out=or_[0:nt, :], in0=or_[0:nt, :], in1=ga_[0:nt, D:2 * D], op=ALU.mult), reads=[bga], writes=[bor])
                mb, bmb = W.bf16t.next()
                k.op("dve", lambda e: e.tensor_tensor(out=mb[0:nt, :], in0=oa_[0:nt, :], in1=or_[0:nt, :], op=ALU.add), reads=[boa, bor], writes=[bmb])
                transpose_to(mb, bmb, nt, 8, lambda c0, c1: W.hT[:, c0:c1, ti * 128:ti * 128 + nt], W.b_hT[ti])

            def body_o(blk, wt, bw):
                for ti, (tok0, nt) in enumerate(tiles):
                    pt, bp = mm_block(lambda c: W.hT[:, c, ti * 128:ti * 128 + nt], 8, wt, bw, 512, nt, [W.b_hT[ti]])
                    st, bst_ = W.half.next()
                    k.op("dve", lambda e: e.tensor_tensor(out=st[0:nt, :], in0=pt[0:nt, :], in1=mod[0:nt, 5 * D + blk * 512:5 * D + (blk + 1) * 512], op=ALU.mult),
                         reads=[bp, b_mod], writes=[bst_])
                    k.op("pool", lambda e: e.tensor_tensor(out=W.xb[0:nt, ti, blk * 512:(blk + 1) * 512], in0=W.xb[0:nt, ti, blk * 512:(blk + 1) * 512],
                                                          in1=st[0:nt, :], op=ALU.add),
                         reads=[bst_, W.b_x[ti]], writes=[W.b_x[ti]])
            stream_w([(lambda blk=blk: load_w(w_out, 8, [(blk * 512, 512, 0)], W.wk8, ("wo", blk))) for blk in range(2)], body_o)
            ffn(tiles, f2_in, f2_out, 6, 7, 8, "f2")
            for ti, (tok0, nt) in enumerate(tiles):
                sm, bsm = W.small.next()
                jt, bj = W.f32t.next()
                xt = W.xb[0:nt, ti, :]
                k.op("act", lambda e: e.activation(out=jt[0:nt, :], in_=xt, func=AF.Square, accum_out=sm[0:nt, 0:1]), reads=[W.b_x[ti]], writes=[bj, bsm])
                k.op("act", lambda e: e.activation(out=sm[0:nt, 1:2], in_=sm[0:nt, 0:1], func=AF.Sqrt, scale=1.0 / D, bias=eps_t[0:nt, 0:1]),
                     reads=[bsm, b_eps], writes=[bsm])
                k.op("dve", lambda e: e.reciprocal(out=sm[0:nt, 2:3], in_=sm[0:nt, 1:2]), reads=[bsm], writes=[bsm])
                k.op("dve", lambda e: e.scalar_tensor_tensor(out=jt[0:nt, :], in0=xt, scalar=sm[0:nt, 2:3], in1=nf_t[0:nt, :], op0=ALU.mult, op1=ALU.mult),
                     reads=[W.b_x[ti], bsm, b_nf], writes=[bj])
                k.dma("sp", y_out[tok0 - yoff:tok0 - yoff + nt, :], jt[0:nt, :], reads=[bj], writes=[bufs_d["out"]])

        phase_d([(NPT, NST)], y_s, NPT)
        compute_mod(0, 128)
        for u in range(4):
            phase_d([(u * 512 + i * 128, 128) for i in range(4)], y_p, 0)
        k.finish_all()
        esD.close()
    print("instructions", k.n_inst, "waits", k.n_wait)
    return nc


def host_tables(j):
    pos = np.concatenate([np.concatenate([np.arange(512) + 512 * (4 * u + j) for u in range(4)]),
                          np.tile(2048 + np.arange(4), 16)]).astype(np.float32)
    inv = (10000.0 ** (-np.arange(32, dtype=np.float32) / 32)).astype(np.float32)
    ang = pos[:, None] * inv[None, :]
    c, s = np.cos(ang), np.sin(ang)
    cc = np.concatenate([c, c], 1)
    ss = np.concatenate([s, s], 1)
    ropek = np.stack([cc, ss], 1).astype(np.float32)
    ropeq = (ropek * 0.125).astype(np.float32)
    angle = (1.0 / (10000.0 ** np.linspace(0.0, 1.0, 128, dtype=np.float32))).astype(np.float32)
    ang2 = pos[:, None] * angle[None, :]
    c2 = np.repeat(np.cos(ang2), 2, axis=1)
    s2 = np.repeat(np.sin(ang2), 2, axis=1)
    retq = np.stack([c2, s2], 1).astype(np.float32)
    retk = (retq / 16.0).astype(np.float32)
    lg = np.log1p(-np.exp2(-5.0 - np.arange(4, dtype=np.float32))).astype(np.float32)
    idx = np.concatenate([np.tile(np.arange(128), 16), np.tile(np.arange(4), 16)]).astype(np.float32)
    L = np.concatenate([np.full(NPT, 128.0), np.full(NST, 4.0)]).astype(np.float32)
    kdec = np.exp((L - 1.0 - idx)[:, None] * lg[None, :])
    qdec = np.exp((idx + 1.0)[:, None] * lg[None, :])
    dec = np.concatenate([kdec, qdec], 1).astype(np.float32)
    kp = np.arange(128)[:, None, None]
    rel = np.arange(16)[None, :, None]
    qi = np.arange(512)[None, None, :]
    maskd = (128 * (rel - 4 * j) + kp <= qi).astype(np.float32)
    msk = np.zeros((128, 4), np.float32)
    msk[:, j] = 1.0
    sI = np.arange(128)[:, None, None].astype(np.float32)
    lI = np.arange(128)[None, None, :].astype(np.float32)
    dmp = np.where(lI >= sI, np.exp(np.maximum(lI - sI, 0.0) * lg[None, :, None]), 0.0).astype(np.float32)
    s6 = np.arange(64)[:, None, None]
    l6 = np.arange(64)[None, None, :]
    dms = np.where((l6 >= s6) & (l6 // 4 == s6 // 4), np.exp(np.maximum(l6 - s6, 0).astype(np.float32) * lg[None, :, None]), 0.0).astype(np.float32)
    seqm = (np.arange(64)[:, None] // 4 == np.arange(16)[None, :]).astype(np.float32)
    row = np.arange(64)
    rh, rc, rq = row // 8, (row // 4) % 2, row % 4
    satab = np.zeros((64, 1032), np.float32)
    for r in range(64):
        satab[r, rh[r] * 128:(rh[r] + 1) * 128] = 1.0
        satab[r, 1024] = 1.0 if rc[r] == 0 else 0.0
        satab[r, 1025] = 1.0 if rc[r] == 1 else 0.0
        satab[r, 1026 + rq[r]] = 1.0
    tri = (np.arange(4)[:, None] <= (np.arange(64)[None, :] % 4)).astype(np.float32)
    return dict(ropeq=ropeq, ropek=ropek, retq=retq, retk=retk, dec=dec, maskd=maskd, msk=msk, dmp=dmp, dms=dms, seqm=seqm, satab=satab, tri=tri)


def make_in_maps(inp, npool=NPOOL_FULL):
    f = lambda a: np.ascontiguousarray(np.asarray(a, dtype=np.float32))
    xpr = f(inp["x_prompt"])
    xsm = f(inp["x_sample"])
    cp = f(inp["c_prompt"])
    csm = f(inp["c_sample"])
    nrm = np.stack([f(inp["norm_ffn1"])[0], f(inp["norm_mix"])[0], f(inp["norm_ffn2"])[0],
                    f(inp["norm_final"]), f(inp["ret_norm_g"])[0]], 0)
    shared = dict(ada_w=f(inp["ada_w"])[0], ada_b=f(inp["ada_b"]), nrm=nrm,
                  ffn1_w_in=f(inp["ffn1_w_in"])[0], ffn1_w_out=f(inp["ffn1_w_out"])[0],
                  ffn2_w_in=f(inp["ffn2_w_in"])[0], ffn2_w_out=f(inp["ffn2_w_out"])[0],
                  w_in=f(inp["w_in"])[0], w_out=f(inp["w_out"])[0], ident=np.eye(128, dtype=np.float32),
                  lam=np.stack([f(inp["lam_q1"])[0], f(inp["lam_k1"])[0], f(inp["lam_q2"])[0], f(inp["lam_k2"])[0]], 0),
                  subln=f(inp["subln_g"]))
    sret = inp["state_ret"]
    ck = np.asarray(inp["cache_k"], dtype=np.float32).reshape(-1, D)
    cv = np.asarray(inp["cache_v"], dtype=np.float32).reshape(-1, D)
    ptab = np.asarray(inp["page_table"], dtype=np.int32)
    maps = []
    for c in range(8):
        s, j = c // 4, c % 4
        xg = xpr[s].reshape(16, 512, D)
        m = dict(shared)
        m["xp"] = np.ascontiguousarray(np.concatenate([xg[4 * u + j] for u in range(4)], 0))
        m["xs"] = np.ascontiguousarray(xsm[16 * c:16 * c + 16].reshape(NST, D))
        m["ctok"] = np.ascontiguousarray(np.concatenate([np.broadcast_to(cp[s], (128, D)),
                                                         np.repeat(csm[16 * c:16 * c + 16], 4, axis=0)], 0))
        m["pt"] = np.ascontiguousarray(ptab[16 * c:16 * c + 16].reshape(1, 256))
        m["cache_k"] = ck
        m["cache_v"] = cv
        m["st_in"] = np.ascontiguousarray(np.asarray(sret[0, 16 * c:16 * c + 16], dtype=np.float32))
        m.update(host_tables(j))
        maps.append(m)
    return maps


def assemble(res):
    y_prompt = np.zeros((2, 8192, D), np.float32)
    k_prompt = np.zeros((1, 2, 8192, 16, 64), np.float32)
    v_prompt = np.zeros((1, 2, 8192, 8, 128), np.float32)
    y_sample = np.zeros((128, 4, D), np.float32)
    k_sample = np.zeros((1, 128, 4, 16, 64), np.float32)
    v_sample = np.zeros((1, 128, 4, 8, 128), np.float32)
    st_p = np.zeros((1, 2, 4, 256, 256), np.float32)
    st_s = np.zeros((1, 128, 4, 256, 256), np.float32)
    for c in range(8):
        s, j = c // 4, c % 4
        r = res[c]
        for u in range(4):
            g = 4 * u + j
            sl = slice(512 * g, 512 * (g + 1))
            y_prompt[s, sl] = r["y_p"][u * 512:(u + 1) * 512]
            k_prompt[0, s, sl] = r["k_p"][u * 512:(u + 1) * 512].reshape(512, 16, 64)
            v_prompt[0, s, sl] = r["v_p"][u * 512:(u + 1) * 512].reshape(512, 8, 128)
        y_sample[16 * c:16 * c + 16] = r["y_s"].reshape(16, 4, D)
        k_sample[0, 16 * c:16 * c + 16] = r["k_s"].reshape(16, 4, 16, 64)
        v_sample[0, 16 * c:16 * c + 16] = r["v_s"].reshape(16, 4, 8, 128)
        st_s[0, 16 * c:16 * c + 16] = r["st_s"]
        if j == 0:
            st_p[0, s] = r["st_p"]
    return (y_prompt, y_sample, k_prompt, v_prompt, st_p, k_sample, v_sample, st_s)


_NC = {}


def kernel(**inputs):
    if "nc" not in _NC:
        _NC["nc"] = build(npool=int(np.asarray(inputs["cache_k"]).shape[1]))
    maps = make_in_maps(inputs)
    res = run_bass_kernel_spmd(_NC["nc"], maps, core_ids=list(range(8)))
    return assemble(res.results)
```

```python
import math
from contextlib import ExitStack
from types import SimpleNamespace
import numpy as np
import concourse.bass as bass
import concourse.mybir as mybir
from concourse.bass_utils import run_bass_kernel_spmd

F32 = mybir.dt.float32
BF16 = mybir.dt.bfloat16
I32 = mybir.dt.int32
AF = mybir.ActivationFunctionType
ALU = mybir.AluOpType
AX = mybir.AxisListType

D = 1024
DFF = 2816
NPOOL_FULL = 2560
NPT = 2048
NST = 64
NTOK = NPT + NST
import os
DBG_KINDS = set(int(c) for c in os.environ.get('DBG_KINDS', '012345678'))
DBG_GS = int(os.environ.get('DBG_GS', '1'))
DO_ATTN = int(os.environ.get('DO_ATTN', '1'))
DO_RET = int(os.environ.get('DO_RET', '1'))
DO_SATT = int(os.environ.get('DO_SATT', '1'))
OPT_PIPE = int(os.environ.get('OPT_PIPE', '1'))
OPT_PROG = int(os.environ.get('OPT_PROG', '0'))
OPT_WC = int(os.environ.get('OPT_WC', '1'))
DBG_SA = int(os.environ.get('DBG_SA', '9'))
DBG_SV = int(os.environ.get('DBG_SV', '0'))
LAM_INIT = 0.8 - 0.6 * math.exp(-0.3 * 0)
GAM = [1.0 - 2.0 ** (-5 - h) for h in range(4)]


class Buf:
    __slots__ = ("name", "w", "r", "excl")

    def __init__(self, name="", excl=False):
        self.name = name
        self.w = {}
        self.r = {}
        self.excl = excl


class Eng:
    def __init__(self, k, key, eng, is_pe=False):
        self.key = key
        self.e = eng
        self.sem = k.nc.alloc_semaphore("sem_" + key)
        self.count = 0
        self.seen = {}
        self.is_pe = is_pe
        self.dsems = []
        self.dvals = []
        self.dnext = 0


class K:
    NDMA = 12

    def __init__(self, nc):
        self.nc = nc
        self.engs = {}
        for key, eng, pe in (("pe", nc.tensor, True), ("dve", nc.vector, False),
                             ("act", nc.scalar, False), ("pool", nc.gpsimd, False),
                             ("sp", nc.sync, False)):
            self.engs[key] = Eng(self, key, eng, pe)
        self.semobj = {key: e.sem for key, e in self.engs.items()}
        self.n_wait = 0
        self.n_inst = 0

    def _deps(self, reads, writes):
        deps = {}
        for b in reads:
            for kk, v in b.w.items():
                if deps.get(kk, 0) < v:
                    deps[kk] = v
        for b in writes:
            for kk, v in b.w.items():
                if deps.get(kk, 0) < v:
                    deps[kk] = v
            for kk, v in b.r.items():
                if deps.get(kk, 0) < v:
                    deps[kk] = v
        return deps

    def _wait(self, E, deps):
        for kk, v in deps.items():
            if kk == E.key and E.is_pe:
                continue
            if E.seen.get(kk, 0) >= v:
                continue
            E.e.wait_ge(self.semobj[kk], v)
            E.seen[kk] = v
            self.n_wait += 1

    def _mark(self, key, val, reads, writes):
        for b in reads:
            if b.r.get(key, 0) < val:
                b.r[key] = val
        for b in writes:
            b.w = {key: val}
            b.r = {}

    def op(self, ek, fn, reads=(), writes=()):
        E = self.engs[ek]
        if any(b.excl for b in reads):
            writes = list(writes) + [b for b in reads if b.excl]
            reads = [b for b in reads if not b.excl]
        self._wait(E, self._deps(reads, writes))
        inst = fn(E.e)
        E.count += 1
        inst.then_inc(E.sem, 1)
        self._mark(E.key, E.count, reads, writes)
        self.n_inst += 1
        return inst

    def dma(self, qk, out, in_, reads=(), writes=(), fn=None, inc=16):
        E = self.engs[qk]
        if not E.dsems:
            for i in range(self.NDMA):
                s = self.nc.alloc_semaphore("dsem_%s_%d" % (qk, i))
                E.dsems.append(s)
                E.dvals.append(0)
                self.semobj[("d", qk, i)] = s
        self._wait(E, self._deps(reads, writes))
        i = E.dnext
        E.dnext = (i + 1) % self.NDMA
        key = ("d", qk, i)
        if E.dvals[i] > 0 and E.seen.get(key, 0) < E.dvals[i]:
            E.e.wait_ge(E.dsems[i], E.dvals[i])
            E.seen[key] = E.dvals[i]
            self.n_wait += 1
        if fn is None:
            inst = E.e.dma_start(out=out, in_=in_)
        else:
            inst = fn(E.e)
        E.dvals[i] += inc
        inst.then_inc(E.dsems[i], inc)
        self._mark(key, E.dvals[i], reads, writes)
        self.n_inst += 1
        return inst

    def coll(self, fn, reads=(), writes=()):
        E = self.engs["pool"]
        self._wait(E, self._deps(reads, writes))
        n = getattr(self, "_ncoll", 0)
        self._ncoll = n + 1
        sem = self.nc.alloc_semaphore("csem_%d" % n)
        key = ("c", n)
        self.semobj[key] = sem
        inst = fn(E.e)
        inst.then_inc(sem)
        self._mark(key, 1, reads, writes)
        self.n_inst += 1
        return inst

    def barrier(self):
        deps = {}
        for key, e in self.engs.items():
            if e.count:
                deps[key] = e.count
            for i, v in enumerate(e.dvals):
                if v:
                    deps[("d", key, i)] = v
        for key, E in self.engs.items():
            self._wait(E, dict(deps))

    def finish_all(self):
        E = self.engs["sp"]
        deps = {}
        for key, e in self.engs.items():
            if e.count:
                deps[key] = e.count
            for i, v in enumerate(e.dvals):
                if v:
                    deps[("d", key, i)] = v
        self._wait(E, deps)


class Rot:
    def __init__(self, items, share=None):
        self.items = items
        self.st = share.st if share is not None else [0]

    def next(self):
        i = self.st[0]
        it = self.items[i]
        self.st[0] = (i + 1) % len(self.items)
        return it


def build(npool=NPOOL_FULL, upto=99):
    nc = bass.Bass("TRN2", target_bir_lowering=False)
    k = K(nc)
    es = ExitStack()

    def din(name, shape, dt=F32):
        return nc.dram_tensor(name, list(shape), dt, kind="ExternalInput").ap()

    def dout(name, shape, dt=F32):
        return nc.dram_tensor(name, list(shape), dt, kind="ExternalOutput").ap()

    def dscr(name, shape, dt):
        return nc.dram_tensor(name, list(shape), dt).ap()

    def sb(name, shape, dt):
        return es.enter_context(nc.sbuf_tensor(name, list(shape), dt))

    def ps(name, shape, dt):
        return es.enter_context(nc.psum_tensor(name, list(shape), dt))

    xp = din("xp", [NPT, D])
    xs = din("xs", [NST, D])
    ctok = din("ctok", [192, D])
    ada_w = din("ada_w", [D, 9 * D])
    ada_b = din("ada_b", [1, 9 * D])
    nrm = din("nrm", [5, D])
    f1_in = din("ffn1_w_in", [D, 2 * DFF])
    f1_out = din("ffn1_w_out", [DFF, D])
    f2_in = din("ffn2_w_in", [D, 2 * DFF])
    f2_out = din("ffn2_w_out", [DFF, D])
    w_in = din("w_in", [D, 9 * D])
    w_out = din("w_out", [D, D])
    ident_d = din("ident", [128, 128])
    ropeq = din("ropeq", [NTOK, 2, 64])
    ropek = din("ropek", [NTOK, 2, 64])
    retq = din("retq", [NTOK, 2, 256])
    retk = din("retk", [NTOK, 2, 256])
    dec = din("dec", [NTOK, 8])
    lam_d = din("lam", [4, 64])
    subln_d = din("subln", [1, 128])
    msk_d = din("msk", [128, 4])
    dmp_d = din("dmp", [128, 4, 128])
    dms_d = din("dms", [64, 4, 64])
    seqm_d = din("seqm", [64, 16])
    maskd = din("maskd", [128, 16, 512])
    st_in = din("st_in", [16, 4, 256, 256])
    pt_d = din("pt", [1, 256], I32)
    satab_d = din("satab", [64, 8 * 128 + 8])
    tri_d = din("tri", [4, 64])
    cache_k2 = din("cache_k", [npool * 128, D])
    cache_v2 = din("cache_v", [npool * 128, D])

    y_p = dout("y_p", [NPT, D])
    y_s = dout("y_s", [NST, D])
    k_p = dout("k_p", [NPT, D])
    v_p = dout("v_p", [NPT, D])
    st_p = dout("st_p", [4, 256, 256])
    k_s = dout("k_s", [NST, D])
    v_s = dout("v_s", [NST, D])
    st_s = dout("st_s", [16, 4, 256, 256])

    qT_s = dscr("qT_s", [8, 128, NTOK], BF16)
    kT_x = [dscr("kT_x%d" % i, [1024, 512], BF16) for i in range(4)]
    kTs_s = dscr("kTs_s", [8, 128, NST], BF16)
    v_x = dscr("v_x", [NPT, D], BF16)
    vs_s = dscr("vs_s", [NST, D], BF16)
    qrT_s = dscr("qrT_s", [8, 128, NTOK], BF16)
    krT_s = dscr("krT_s", [8, 128, NTOK], BF16)
    krd_s = dscr("krd_s", [NTOK, D], BF16)
    vr_s = dscr("vr_s", [NTOK, D], BF16)
    gate_s = dscr("gate_s", [NTOK, 3 * D], F32)
    xmid_s = dscr("xmid_s", [NTOK, D], F32)
    E_x = dscr("E_x", [4, 128, 2048], F32)
    kT_g = [dscr("kT_g%d" % i, [4 * 1024, 512], BF16) for i in range(4)]
    v_g = [dscr("v_g%d" % i, [4 * 512, D], BF16) for i in range(4)]
    E_g = [dscr("E_g%d" % i, [4 * 128, 2048], F32) for i in range(4)]
    oa_s = dscr("oa_s", [NTOK, D], F32)
    or_s = dscr("or_s", [NTOK, D], F32)
    attn_s = dscr("attn_s", [NST, D], F32)
    bx_kT = [Buf() for _ in range(4)]
    bx_v = [Buf() for _ in range(4)]
    bx_E = [Buf() for _ in range(4)]
    bufs_d = {n: Buf(n) for n in ("qT", "kT", "kTs", "v", "vs", "qrT", "krT", "krd", "vr", "gate", "xmid", "E", "out", "oa", "or", "attn")}

    with es:
        idf = sb("idf", [128, 128], F32)
        idb = sb("idb", [128, 128], BF16)
        b_id = Buf()
        k.dma("sp", idf[:], ident_d, writes=[b_id])
        k.op("dve", lambda e: e.tensor_copy(out=idb[:], in_=idf[:]), reads=[b_id], writes=[b_id])
        eps_t = sb("eps_t", [128, 1], F32)
        eps5_t = sb("eps5_t", [128, 1], F32)
        ones_b = sb("ones_b", [128, 1], BF16)
        b_eps = Buf()

        mod = sb("mod", [128, 9 * D], F32)
        b_mod = Buf("mod")

        pmm = Rot([(ps("pmm_%d" % i, [128, 512], F32), Buf(excl=True)) for i in range(4)])
        ptr = Rot([(ps("ptr_%d" % i, [128, 8, 128], BF16), Buf(excl=True)) for i in range(2)])
        pl = [(ps("pl_%d" % i, [128, 512], F32), Buf(excl=True)) for i in range(2)]
        TB = 4
        W = SimpleNamespace()

        def mk_work(es_, tag, phase_a):
            def sbw(name, shape, dt):
                return es_.enter_context(nc.sbuf_tensor(name + tag, list(shape), dt))
            _wk = [(sbw("wk_%d" % i, [128, 5632], BF16), Buf()) for i in range(3)]
            W.wk8 = Rot([(t[:, 0:4096].rearrange("p (c n) -> p c n", n=512), b) for (t, b) in _wk])
            W.wk22 = Rot([(t[:, :].rearrange("p (c n) -> p c n", n=256), b) for (t, b) in _wk], share=W.wk8)
            W.xb = sbw("xb", [128, TB, D], F32)
            W.b_x = [Buf() for _ in range(TB)]
            W.hT = sbw("hT", [128, 8, TB * 128], BF16)
            W.b_hT = [Buf() for _ in range(TB)]
            W.gtok = sbw("gtok", [128, TB, DFF], BF16)
            W.b_g = [Buf() for _ in range(TB)]
            W.gT = sbw("gT", [128, 22, TB * 128], BF16)
            W.b_gT = [Buf() for _ in range(TB)]
            W.f32t = Rot([(sbw("f32t_%d" % i, [128, D], F32), Buf()) for i in range(2)])
            W.bf16t = Rot([(sbw("bf16t_%d" % i, [128, D], BF16), Buf()) for i in range(3)])
            W.half = Rot([(sbw("half_%d" % i, [128, 512], F32), Buf()) for i in range(4)])
            W.halfb = Rot([(sbw("halfb_%d" % i, [128, 512], BF16), Buf()) for i in range(3)])
            W.small = Rot([(sbw("small_%d" % i, [128, 8], F32), Buf()) for i in range(6)])
            W.biasb = Rot([(sbw("biasb_%d" % i, [128, 512], F32), Buf()) for i in range(2)])
            if phase_a:
                W.trT = Rot([(sbw("trT_%d" % i, [128, 4, 128], BF16), Buf()) for i in range(3)])
                W.tabs = Rot([(sbw("tabs_%d" % i, [128, 2, 256], F32), Buf()) for i in range(2)])
                W.krd_t = sbw("krd_t", [128, TB, D], BF16)
                W.b_krd = [Buf() for _ in range(TB)]
                W.vr_t = sbw("vr_t", [128, TB, D], BF16)
                W.b_vr = [Buf() for _ in range(TB)]
                W.Eacc = sbw("Eacc", [128, 4, 2, 256], F32)
                W.b_E = Buf()
                W.dec_t = sbw("dec_t", [128, TB, 8], F32)
                W.b_dec = [Buf() for _ in range(TB)]

        wcache = {}

        def load_w(Wd, KC, cols, rot, tag=None):
            wt, bw = rot.next()
            width = sum(wd for (_, wd, _) in cols)
            if not OPT_WC:
                tag = None
            if tag is not None and tag in wcache:
                scr, bscr = wcache[tag]
                k.dma("sp", wt[:, 0:KC, 0:width], scr.rearrange("p (c n) -> p c n", n=width), reads=[bscr], writes=[bw])
                return wt, bw
            for (c0, wd, d0) in cols:
                k.dma("pool", wt[:, 0:KC, d0:d0 + wd],
                      Wd[:, c0:c0 + wd].rearrange("(c p) n -> p c n", p=128), writes=[bw])
            if tag is not None:
                scr = dscr("wc_%s_%d" % (tag[0], tag[1]), [128, KC * width], BF16)
                bscr = Buf()
                k.dma("sp", scr.rearrange("p (c n) -> p c n", n=width), wt[:, 0:KC, 0:width], reads=[bw], writes=[bscr])
                wcache[tag] = (scr, bscr)
            return wt, bw

        def stream_w(loads, body):
            q = []
            n = len(loads)
            for i in range(min(2, n)):
                q.append(loads[i]())
            for i in range(n):
                if i + 2 < n:
                    q.append(loads[i + 2]())
                body(i, *q[i])

        def mm_block(lhs_fn, KC, wt, bw, width, nt, lbufs):
            pt, bp = pmm.next()
            for c in range(KC):
                k.op("pe", lambda e: e.matmul(pt[0:nt, 0:width], lhsT=lhs_fn(c), rhs=wt[:, c, 0:width],
                                              start=(c == 0), stop=(c == KC - 1)),
                     reads=list(lbufs) + [bw], writes=[bp])
            return pt, bp

        def transpose_to(src, bsrc, nt, ncol, dst_fn, bdst, f32=False):
            for c0 in range(0, ncol, 8):
                c1 = min(ncol, c0 + 8)
                pt, bp = ptr.next()
                for c in range(c0, c1):
                    k.op("pe", lambda e: e.transpose(out=pt[:, c - c0, 0:nt], in_=src[0:nt, c * 128:(c + 1) * 128],
                                                     identity=idb[0:nt, 0:nt]),
                         reads=[bsrc, b_id], writes=[bp])
                k.op("act", lambda e: e.copy(out=dst_fn(c0, c1), in_=pt[:, 0:c1 - c0, 0:nt]),
                     reads=[bp], writes=[bdst])

        def norm_mod(xt, bx, nt, gi, si, dst, bdst):
            sm, bsm = W.small.next()
            jt, bj = W.f32t.next()
            k.op("act", lambda e: e.activation(out=jt[0:nt, :], in_=xt, func=AF.Square, accum_out=sm[0:nt, 0:1]),
                 reads=[bx], writes=[bj, bsm])
            k.op("act", lambda e: e.activation(out=sm[0:nt, 1:2], in_=sm[0:nt, 0:1], func=AF.Sqrt, scale=1.0 / D, bias=eps_t[0:nt, 0:1]),
                 reads=[bsm, b_eps], writes=[bsm])
            k.op("dve", lambda e: e.reciprocal(out=sm[0:nt, 2:3], in_=sm[0:nt, 1:2]), reads=[bsm], writes=[bsm])
            k.op("dve", lambda e: e.scalar_tensor_tensor(out=jt[0:nt, :], in0=xt, scalar=sm[0:nt, 2:3],
                                                        in1=mod[0:nt, gi * D:(gi + 1) * D], op0=ALU.mult, op1=ALU.mult),
                 reads=[bx, bsm, b_mod], writes=[bj])
            k.op("pool", lambda e: e.tensor_tensor(out=dst, in0=jt[0:nt, :], in1=mod[0:nt, si * D:(si + 1) * D], op=ALU.add),
                 reads=[bj, b_mod], writes=[bdst])

        def compute_mod(row0, nt):
            ct, bc = W.f32t.next()
            k.dma("sp", ct[0:nt, :], ctok[row0:row0 + nt, :], writes=[bc])
            cb, bcb = W.bf16t.next()
            k.op("act", lambda e: e.activation(out=cb[0:nt, :], in_=ct[0:nt, :], func=AF.Silu), reads=[bc], writes=[bcb])
            cTt = sb_cT
            transpose_to(cb, bcb, nt, 8, lambda c0, c1: cTt[:, c0:c1, 0:nt], b_cT)
            def body(blk, wt, bw):
                bb, bbb = W.biasb.next()
                k.dma("sp", bb[0:nt, :], ada_b[:, blk * 512:(blk + 1) * 512].partition_broadcast(nt), writes=[bbb])
                pt, bp = mm_block(lambda c: cTt[:, c, 0:nt], 8, wt, bw, 512, nt, [b_cT])
                k.op("dve", lambda e: e.tensor_tensor(out=mod[0:nt, blk * 512:(blk + 1) * 512], in0=pt[0:nt, :], in1=bb[0:nt, :], op=ALU.add),
                     reads=[bp, bbb], writes=[b_mod])
            stream_w([(lambda blk=blk: load_w(ada_w, 8, [(blk * 512, 512, 0)], W.wk8, ("ada", blk))) for blk in range(18)], body)
            for (si, gi) in ((1, 0), (4, 1), (7, 2)):
                nt_, bn_ = W.f32t.next()
                k.dma("sp", nt_[0:nt, :], nrm[gi:gi + 1, :].partition_broadcast(nt), writes=[bn_])
                k.op("dve", lambda e: e.scalar_tensor_tensor(out=mod[0:nt, si * D:(si + 1) * D], in0=mod[0:nt, si * D:(si + 1) * D],
                                                            scalar=1.0, in1=nt_[0:nt, :], op0=ALU.add, op1=ALU.mult),
                     reads=[b_mod, bn_], writes=[b_mod])
            for gi in (2, 8):
                k.op("pool", lambda e: e.tensor_scalar(out=mod[0:nt, gi * D:(gi + 1) * D], in0=mod[0:nt, gi * D:(gi + 1) * D],
                                                       scalar1=0.5, scalar2=None, op0=ALU.mult),
                     reads=[b_mod], writes=[b_mod])

        sb_cT = sb("cT", [128, 8, 128], BF16)
        b_cT = Buf()

        def ffn(tiles, Win, Wout, sh_i, sc_i, gt_i, wtag):
            nT = len(tiles)
            for ti, (tok0, nt) in enumerate(tiles):
                hb, bhb = W.bf16t.next()
                norm_mod(W.xb[0:nt, ti, :], W.b_x[ti], nt, sc_i, sh_i, hb[0:nt, :], bhb)
                transpose_to(hb, bhb, nt, 8, lambda c0, c1: W.hT[:, c0:c1, ti * 128:ti * 128 + nt], W.b_hT[ti])
            def body_in(blk, wt, bw):
                for ti, (tok0, nt) in enumerate(tiles):
                    pt, bp = mm_block(lambda c: W.hT[:, c, ti * 128:ti * 128 + nt], 8, wt, bw, 512, nt, [W.b_hT[ti]])
                    st, bst = W.half.next()
                    k.op("act", lambda e: e.activation(out=st[0:nt, 0:256], in_=pt[0:nt, 0:256], func=AF.Silu), reads=[bp], writes=[bst])
                    k.op("dve", lambda e: e.tensor_tensor(out=W.gtok[0:nt, ti, blk * 256:(blk + 1) * 256], in0=pt[0:nt, 256:512],
                                                         in1=st[0:nt, 0:256], op=ALU.mult),
                         reads=[bp, bst], writes=[W.b_g[ti]])
            stream_w([(lambda blk=blk: load_w(Win, 8, [(blk * 256, 256, 0), (DFF + blk * 256, 256, 256)], W.wk8, (wtag + "i", blk))) for blk in range(11)], body_in)
            for ti, (tok0, nt) in enumerate(tiles):
                transpose_to(W.gtok[:, ti, :], W.b_g[ti], nt, 22, lambda c0, c1: W.gT[:, c0:c1, ti * 128:ti * 128 + nt], W.b_gT[ti])
            def body_out(blk, wt, bw):
                for ti, (tok0, nt) in enumerate(tiles):
                    pt, bp = mm_block(lambda c: W.gT[:, c, ti * 128:ti * 128 + nt], 22, wt, bw, 256, nt, [W.b_gT[ti]])
                    st, bst = W.half.next()
                    k.op("dve", lambda e: e.tensor_tensor(out=st[0:nt, 0:256], in0=pt[0:nt, 0:256],
                                                         in1=mod[0:nt, gt_i * D + blk * 256:gt_i * D + (blk + 1) * 256], op=ALU.mult),
                         reads=[bp, b_mod], writes=[bst])
                    k.op("pool", lambda e: e.tensor_tensor(out=W.xb[0:nt, ti, blk * 256:(blk + 1) * 256], in0=W.xb[0:nt, ti, blk * 256:(blk + 1) * 256],
                                                          in1=st[0:nt, 0:256], op=ALU.add),
                         reads=[bst, W.b_x[ti]], writes=[W.b_x[ti]])
            stream_w([(lambda blk=blk: load_w(Wout, 22, [(blk * 256, 256, 0)], W.wk22, (wtag + "o", blk))) for blk in range(4)], body_out)

        def rotate(pt, bp, nt, tabd, tok0, nh, pairmode, out32, bout):
            WID = 64 if not pairmode else 256
            tb, btb = W.tabs.next()
            k.dma("sp", tb[0:nt, :, 0:WID], tabd[tok0:tok0 + nt, :, :], writes=[btb])
            xc, bxc = W.half.next()
            k.op("act", lambda e: e.copy(out=xc[0:nt, :], in_=pt[0:nt, :]), reads=[bp], writes=[bxc])
            A, bA = W.half.next()
            B, bB = W.half.next()
            xv = xc[0:nt, :].rearrange("p (h w) -> p h w", w=WID)
            cb_ = tb[0:nt, 0, 0:WID].unsqueeze(1).broadcast_to([nt, nh, WID])
            sb_ = tb[0:nt, 1, 0:WID].unsqueeze(1).broadcast_to([nt, nh, WID])
            k.op("pool", lambda e: e.tensor_tensor(out=A[0:nt, :].rearrange("p (h w) -> p h w", w=WID), in0=xv, in1=cb_, op=ALU.mult),
                 reads=[bxc, btb], writes=[bA])
            k.op("dve", lambda e: e.tensor_tensor(out=B[0:nt, :].rearrange("p (h w) -> p h w", w=WID), in0=xv, in1=sb_, op=ALU.mult),
                 reads=[bxc, btb], writes=[bB])
            if not pairmode:
                def v(t, j):
                    return t[0:nt, :].rearrange("p (h two w) -> p h two w", two=2, w=32)[:, :, j, :]
            else:
                def v(t, j):
                    return t[0:nt, :].rearrange("p (h w two) -> p h w two", two=2, w=128)[:, :, :, j]
            k.op("dve", lambda e: e.tensor_tensor(out=v(out32, 0), in0=v(A, 0), in1=v(B, 1), op=ALU.subtract),
                 reads=[bA, bB], writes=[bout])
            k.op("pool", lambda e: e.tensor_tensor(out=v(out32, 1), in0=v(A, 1), in1=v(B, 0), op=ALU.add),
                 reads=[bA, bB, bout], writes=[bout])

        def tr4_to_dram(srcb, bsrc, nt, dram3, h0, tok0, bdram):
            tt, btt = W.trT.next()
            pt, bp = ptr.next()
            for c in range(4):
                k.op("pe", lambda e: e.transpose(out=pt[:, c, 0:nt], in_=srcb[0:nt, c * 128:(c + 1) * 128], identity=idb[0:nt, 0:nt]),
                     reads=[bsrc, b_id], writes=[bp])
            k.op("act", lambda e: e.copy(out=tt[:, :, 0:nt], in_=pt[:, 0:4, 0:nt]), reads=[bp], writes=[btt])
            k.dma("sp", dram3[h0:h0 + 4, :, tok0:tok0 + nt].rearrange("h p t -> p h t"), tt[:, :, 0:nt], reads=[btt], writes=[bdram])

        def mixer_in(tiles, is_sample):
            nT = len(tiles)
            for ti, (tok0, nt) in enumerate(tiles):
                hb, bhb = W.bf16t.next()
                norm_mod(W.xb[0:nt, ti, :], W.b_x[ti], nt, 4, 3, hb[0:nt, :], bhb)
                transpose_to(hb, bhb, nt, 8, lambda c0, c1: W.hT[:, c0:c1, ti * 128:ti * 128 + nt], W.b_hT[ti])
                k.dma("sp", W.dec_t[0:nt, ti, :], dec[tok0:tok0 + nt, :], writes=[W.b_dec[ti]])
                k.dma("sp", xmid_s[tok0:tok0 + nt, :], W.xb[0:nt, ti, :], reads=[W.b_x[ti]], writes=[bufs_d["xmid"]])
            def body_mix(blk, wt, bw):
                kind, sub = blk // 2, blk % 2
                if kind not in DBG_KINDS:
                    return
                for ti, (tok0, nt) in enumerate(tiles):
                    pt, bp = mm_block(lambda c: W.hT[:, c, ti * 128:ti * 128 + nt], 8, wt, bw, 512, nt, [W.b_hT[ti]])
                    cs = slice(sub * 512, (sub + 1) * 512)
                    if kind in (0, 1):
                        r32, br = W.half.next()
                        rotate(pt, bp, nt, ropeq if kind == 0 else ropek, tok0, 8, False, r32, br)
                        rb, brb = W.halfb.next()
                        k.op("act", lambda e: e.copy(out=rb[0:nt, :], in_=r32[0:nt, :]), reads=[br], writes=[brb])
                        if kind == 0:
                            tr4_to_dram(rb, brb, nt, qT_s, sub * 4, tok0, bufs_d["qT"])
                        else:
                            if is_sample:
                                k.dma("sp", k_s[tok0 - NPT:tok0 - NPT + nt, cs], r32[0:nt, :], reads=[br], writes=[bufs_d["out"]])
                                tr4_to_dram(rb, brb, nt, kTs_s, sub * 4, tok0 - NPT, bufs_d["kTs"])
                            else:
                                k.dma("sp", k_p[tok0:tok0 + nt, cs], r32[0:nt, :], reads=[br], writes=[bufs_d["out"]])
                                tr4_to_dram(rb, brb, nt, kT_x[tok0 // 512].rearrange("(h p) t -> h p t", p=128), sub * 4, tok0 % 512, bx_kT[tok0 // 512])
                    elif kind == 2:
                        r32, br = W.half.next()
                        k.op("act", lambda e: e.copy(out=r32[0:nt, :], in_=pt[0:nt, :]), reads=[bp], writes=[br])
                        rb, brb = W.halfb.next()
                        k.op("dve", lambda e: e.tensor_copy(out=rb[0:nt, :], in_=pt[0:nt, :]), reads=[bp], writes=[brb])
                        if is_sample:
                            k.dma("sp", v_s[tok0 - NPT:tok0 - NPT + nt, cs], r32[0:nt, :], reads=[br], writes=[bufs_d["out"]])
                            k.dma("sp", vs_s[tok0 - NPT:tok0 - NPT + nt, cs], rb[0:nt, :], reads=[brb], writes=[bufs_d["vs"]])
                        else:
                            k.dma("sp", v_p[tok0:tok0 + nt, cs], r32[0:nt, :], reads=[br], writes=[bufs_d["out"]])
                            k.dma("sp", v_x[tok0:tok0 + nt, cs], rb[0:nt, :], reads=[brb], writes=[bx_v[tok0 // 512]])
                    elif kind in (3, 4):
                        r32, br = W.half.next()
                        rotate(pt, bp, nt, retq if kind == 3 else retk, tok0, 2, True, r32, br)
                        rb, brb = W.halfb.next()
                        k.op("act", lambda e: e.copy(out=rb[0:nt, :], in_=r32[0:nt, :]), reads=[br], writes=[brb])
                        tr4_to_dram(rb, brb, nt, qrT_s if kind == 3 else krT_s, sub * 4, tok0, bufs_d["qrT" if kind == 3 else "krT"])
                        if kind == 4:
                            for hh in range(2):
                                h = sub * 2 + hh
                                k.op("dve", lambda e: e.tensor_scalar(out=W.krd_t[0:nt, ti, h * 256:(h + 1) * 256], in0=r32[0:nt, hh * 256:(hh + 1) * 256],
                                                                     scalar1=W.dec_t[0:nt, ti, h:h + 1], scalar2=None, op0=ALU.mult),
                                     reads=[br, W.b_dec[ti]], writes=[W.b_krd[ti]])
                            if sub == 1:
                                k.dma("sp", krd_s[tok0:tok0 + nt, :], W.krd_t[0:nt, ti, :], reads=[W.b_krd[ti]], writes=[bufs_d["krd"]])
                    elif kind == 5:
                        k.op("act", lambda e: e.copy(out=W.vr_t[0:nt, ti, cs], in_=pt[0:nt, :]), reads=[bp], writes=[W.b_vr[ti]])
                        if sub == 1:
                            k.dma("sp", vr_s[tok0:tok0 + nt, :], W.vr_t[0:nt, ti, :], reads=[W.b_vr[ti]], writes=[bufs_d["vr"]])
                    else:
                        r32, br = W.half.next()
                        k.op("act", lambda e: e.activation(out=r32[0:nt, :], in_=pt[0:nt, :], func=AF.Silu if kind == 6 else AF.Sigmoid),
                             reads=[bp], writes=[br])
                        gc = (kind - 6) * D + sub * 512
                        k.dma("sp", gate_s[tok0:tok0 + nt, gc:gc + 512], r32[0:nt, :], reads=[br], writes=[bufs_d["gate"]])
            stream_w([(lambda blk=blk: load_w(w_in, 8, [(blk * 512, 512, 0)], W.wk8, ("win", blk))) for blk in range(18)], body_mix)

        def group_state(u):
            for n in range(4):
                for h in range(4):
                    pt, bp = pmm.next()
                    for dt_ in range(2):
                        k.op("pe", lambda e: e.matmul(pt[:, dt_ * 256:(dt_ + 1) * 256],
                                                      lhsT=W.krd_t[:, n, h * 256 + dt_ * 128:h * 256 + (dt_ + 1) * 128],
                                                      rhs=W.vr_t[:, n, h * 256:(h + 1) * 256], start=True, stop=True),
                             reads=[W.b_krd[n], W.b_vr[n]], writes=[bp])
                    Ev = W.Eacc[:, h, :, :].rearrange("p a b -> p (a b)")
                    if n == 0:
                        k.op("act", lambda e: e.copy(out=Ev, in_=pt[:, :]), reads=[bp], writes=[W.b_E])
                    else:
                        k.op("dve", lambda e: e.scalar_tensor_tensor(out=Ev, in0=Ev, scalar=float(GAM[h] ** 128), in1=pt[:, :],
                                                                    op0=ALU.mult, op1=ALU.add),
                             reads=[bp, W.b_E], writes=[W.b_E])
            k.dma("sp", E_x[u], W.Eacc[:].rearrange("p h a b -> p (h a b)"), reads=[W.b_E], writes=[bx_E[u]])

        groups = [[0, 1, 2, 3], [4, 5, 6, 7]]
        bg = {n: [Buf() for _ in range(4)] for n in ("kT", "v", "E")}

        def exchange(u):
            k.coll(lambda e: e.collective_compute("AllGather", ALU.bypass, replica_groups=groups,
                                                  ins=[kT_x[u].opt()], outs=[kT_g[u].opt()]),
                   reads=[bx_kT[u]], writes=[bg["kT"][u]])
            k.coll(lambda e: e.collective_compute("AllGather", ALU.bypass, replica_groups=groups,
                                                  ins=[v_x[u * 512:(u + 1) * 512, :].opt()], outs=[v_g[u].opt()]),
                   reads=[bx_v[u]], writes=[bg["v"][u]])
            k.coll(lambda e: e.collective_compute("AllGather", ALU.bypass, replica_groups=groups,
                                                  ins=[E_x[u].opt()], outs=[E_g[u].opt()]),
                   reads=[bx_E[u]], writes=[bg["E"][u]])

        k.op("pool", lambda e: e.memset(eps_t[:], 1e-6), writes=[b_eps])
        k.op("pool", lambda e: e.memset(eps5_t[:], 1e-5), writes=[b_eps])
        k.op("pool", lambda e: e.memset(ones_b[:], 1.0), writes=[b_eps])
        esA = ExitStack()
        mk_work(esA, "a", True)
        compute_mod(0, 128)
        for u in range(4):
            tiles = [(u * 512 + i * 128, 128) for i in range(4)]
            for ti, (tok0, nt) in enumerate(tiles):
                k.dma("sp", W.xb[0:nt, ti, :], xp[tok0:tok0 + nt, :], writes=[W.b_x[ti]])
            ffn(tiles, f1_in, f1_out, 0, 1, 2, "f1")
            mixer_in(tiles, False)
            group_state(u)
            if OPT_PROG:
                exchange(u)
        compute_mod(128, 64)
        tiles = [(NPT, NST)]
        k.dma("sp", W.xb[0:NST, 0, :], xs[:, :], writes=[W.b_x[0]])
        ffn(tiles, f1_in, f1_out, 0, 1, 2, "f1")
        mixer_in(tiles, True)
        k.barrier()
        esA.close()
        if not OPT_PROG:
            for u in range(4):
                exchange(u)

        esC = ExitStack()
        cur = [es]

        def sbc(name, shape, dt):
            return cur[0].enter_context(nc.sbuf_tensor(name, list(shape), dt))

        def rot(name, shape, dt, n):
            return Rot([(sbc("%s_%d" % (name, i), shape, dt), Buf()) for i in range(n)])

        smallc = rot("smallc", [128, 8], F32, 8)
        lamv = sbc("lamv", [128, 4, 64], F32)
        b_lam = Buf()
        k.dma("sp", lamv[:], lam_d.partition_broadcast(128), writes=[b_lam])
        lam_t = sbc("lam_t", [128, 8], F32)
        lj = sbc("lj", [128, 64], F32)
        for i in range(2):
            k.op("dve", lambda e: e.tensor_tensor(out=lj[:], in0=lamv[:, 2 * i, :], in1=lamv[:, 2 * i + 1, :], op=ALU.mult),
                 reads=[b_lam], writes=[b_lam])
            k.op("dve", lambda e: e.tensor_reduce(out=lam_t[:, i:i + 1], in_=lj[:], axis=AX.X, op=ALU.add), reads=[b_lam], writes=[b_lam])
        k.op("act", lambda e: e.activation(out=lam_t[:, 2:4], in_=lam_t[:, 0:2], func=AF.Exp), reads=[b_lam], writes=[b_lam])
        k.op("dve", lambda e: e.tensor_tensor(out=lam_t[:, 4:5], in0=lam_t[:, 3:4], in1=lam_t[:, 2:3], op=ALU.subtract), reads=[b_lam], writes=[b_lam])
        k.op("dve", lambda e: e.tensor_scalar(out=lam_t[:, 4:5], in0=lam_t[:, 4:5], scalar1=-LAM_INIT, scalar2=None, op0=ALU.add), reads=[b_lam], writes=[b_lam])
        nlam = lam_t[:, 4:5]
        subg = sbc("subg", [128, 128], F32)
        b_subg = Buf()
        k.dma("sp", subg[:], subln_d.partition_broadcast(128), writes=[b_subg])
        k.op("dve", lambda e: e.tensor_scalar(out=subg[:], in0=subg[:], scalar1=1.0 - LAM_INIT, scalar2=None, op0=ALU.mult), reads=[b_subg], writes=[b_subg])
        rng = sbc("rng", [128, D], F32)
        b_rng = Buf()
        k.dma("sp", rng[:], nrm[4:5, :].partition_broadcast(128), writes=[b_rng])

        def subln(attn, battn, nt, dst, bdst):
            sm, bsm = smallc.next()
            jt, bj = junk.next()
            k.op("act", lambda e: e.activation(out=jt[0:nt, 0:128], in_=attn, func=AF.Square, accum_out=sm[0:nt, 0:1]),
                 reads=[battn], writes=[bj, bsm])
            k.op("act", lambda e: e.activation(out=sm[0:nt, 1:2], in_=sm[0:nt, 0:1], func=AF.Sqrt, scale=1.0 / 128, bias=eps5_t[0:nt, 0:1]),
                 reads=[bsm, b_eps], writes=[bsm])
            k.op("dve", lambda e: e.reciprocal(out=sm[0:nt, 2:3], in_=sm[0:nt, 1:2]), reads=[bsm], writes=[bsm])
            k.op("dve", lambda e: e.scalar_tensor_tensor(out=dst, in0=attn, scalar=sm[0:nt, 2:3], in1=subg[0:nt, :], op0=ALU.mult, op1=ALU.mult),
                 reads=[battn, bsm, b_subg], writes=[bdst])

        junk = rot("junk", [128, 256], F32, 2)
        cur[0] = esC

        if DO_ATTN:
            KT = sbc("KT", [128, 4, 4, 512], BF16)
            b_KT = Buf()
            Vt = sbc("Vt", [128, 4, 4, 4, 128], BF16)
            b_Vt = Buf()
            QT = sbc("QT", [128, NPT], BF16)
            b_QT = Buf()
            maskt = sbc("maskt", [128, 16, 512], BF16)
            b_mk = Buf()
            for r4 in range(4):
                k.dma("pool", maskt[:, r4 * 4:(r4 + 1) * 4, :], maskd[:, r4 * 4:(r4 + 1) * 4, :], writes=[b_mk])
            PT = rot("PT", [128, 512], BF16, 3)
            Osb = [(sbc("Osb%d" % c, [128, 512], F32), Buf()) for c in range(2)]
            lsb = sbc("lsb", [1, 2, 512], F32)
            b_lsb = Buf()
            t12 = rot("t12", [128, 128], F32, 4)
            oat = rot("oat", [128, 128], F32, 3)
            Lacc = [(sbc("Lacc%d" % c, [128, 512], F32), Buf()) for c in range(2)]
            ones_f = sbc("ones_f", [128, 1], F32)
            b_1f = Buf()
            k.op("pool", lambda e: e.memset(ones_f[:], 1.0), writes=[b_1f])
            Sbank = [pmm.items[0], pmm.items[1]]
            Obank = [pmm.items[2], pmm.items[3]]
            sidx = 0
            for h in range(8):
                k.dma("sp", QT[:, :], qT_s[h, :, 0:NPT], reads=[bufs_d["qT"]], writes=[b_QT])
                for jj in range(4):
                    for u in range(4):
                        r0 = jj * 1024 + h * 128
                        k.dma("sp", KT[:, u, jj, :], kT_g[u][r0:r0 + 128, :], reads=[bg["kT"][u]], writes=[b_KT])
                        k.dma("sp", Vt[:, u, jj, :, :],
                              v_g[u][jj * 512:(jj + 1) * 512, h * 128:(h + 1) * 128].rearrange("(kb p) d -> p kb d", p=128),
                              reads=[bg["v"][u]], writes=[b_Vt])
                for u in range(4):
                    nB = 16 * u + 16
                    its = [(B, c) for B in range(nB) for c in range(2)]

                    def emit_qk(n):
                        B, c = its[n]
                        G, kb = B // 4, B % 4
                        gu, gj = G // 4, G % 4
                        pss, bss = Sbank[n % 2]
                        k.op("pe", lambda e: e.matmul(pss[:, :], lhsT=KT[c * 64:(c + 1) * 64, gu, gj, kb * 128:(kb + 1) * 128],
                                                      rhs=QT[c * 64:(c + 1) * 64, u * 512:(u + 1) * 512], start=True, stop=True),
                             reads=[b_KT, b_QT], writes=[bss])

                    emit_qk(0)
                    for n, (B, c) in enumerate(its):
                        if OPT_PIPE and n + 1 < len(its):
                            emit_qk(n + 1)
                        if not OPT_PIPE and n > 0:
                            emit_qk(n)
                        G, kb = B // 4, B % 4
                        gu, gj = G // 4, G % 4
                        pss, bss = Sbank[n % 2]
                        pt_, bpt = PT.next()
                        k.op("act", lambda e: e.activation(out=pt_[:, :], in_=pss[:, :], func=AF.Exp), reads=[bss], writes=[bpt])
                        if B >= 16 * u:
                            k.op("pool", lambda e: e.tensor_tensor(out=pt_[:, :], in0=pt_[:, :], in1=maskt[:, B - 16 * u, :], op=ALU.mult),
                                 reads=[b_mk], writes=[bpt])
                        k.op("pe", lambda e: e.matmul(Obank[c][0][:, :], lhsT=Vt[:, gu, gj, kb, :], rhs=pt_[:, :],
                                                      start=(B == 0), stop=(B == nB - 1)),
                             reads=[b_Vt, bpt], writes=[Obank[c][1]])
                        if B == 0:
                            k.op("dve", lambda e: e.tensor_copy(out=Lacc[c][0][:, :], in_=pt_[:, :]), reads=[bpt], writes=[Lacc[c][1]])
                        else:
                            k.op("dve", lambda e: e.tensor_tensor(out=Lacc[c][0][:, :], in0=Lacc[c][0][:, :], in1=pt_[:, :], op=ALU.add),
                                 reads=[bpt], writes=[Lacc[c][1]])
                    for c in range(2):
                        k.op("pe", lambda e: e.matmul(pl[c][0][0:1, :], lhsT=ones_f[:, 0:1], rhs=Lacc[c][0][:, :], start=True, stop=True),
                             reads=[Lacc[c][1], b_1f], writes=[pl[c][1]])
                    for c in range(2):
                        k.op("act" if c == 0 else "dve",
                             (lambda e: e.copy(out=Osb[c][0][:, :], in_=Obank[c][0][:, :])) if c == 0 else
                             (lambda e: e.tensor_copy(out=Osb[c][0][:, :], in_=Obank[c][0][:, :])),
                             reads=[Obank[c][1]], writes=[Osb[c][1]])
                        k.op("dve", lambda e: e.tensor_copy(out=lsb[0:1, c, :], in_=pl[c][0][0:1, :]), reads=[pl[c][1]], writes=[b_lsb])
                    for qt in range(4):
                        pss, bss = Sbank[sidx]
                        sidx ^= 1
                        qs = slice(qt * 128, (qt + 1) * 128)
                        for c in range(2):
                            k.op("pe", lambda e: e.transpose(out=pss[:, c * 128:(c + 1) * 128], in_=Osb[c][0][:, qs], identity=idf[:, :]),
                                 reads=[Osb[c][1], b_id], writes=[bss])
                            k.op("pe", lambda e: e.transpose(out=pss[:, 256 + c:257 + c], in_=lsb[0:1, c, qs], identity=idf[0:1, 0:1]),
                                 reads=[b_lsb, b_id], writes=[bss])
                        sm, bsm = smallc.next()
                        k.op("dve", lambda e: e.reciprocal(out=sm[:, 0:2], in_=pss[:, 256:258]), reads=[bss], writes=[bsm])
                        t1, bt1 = t12.next()
                        t2, bt2 = t12.next()
                        k.op("dve", lambda e: e.tensor_scalar(out=t1[:, :], in0=pss[:, 0:128], scalar1=sm[:, 0:1], scalar2=None, op0=ALU.mult),
                             reads=[bss, bsm], writes=[bt1])
                        k.op("dve", lambda e: e.tensor_scalar(out=t2[:, :], in0=pss[:, 128:256], scalar1=sm[:, 1:2], scalar2=None, op0=ALU.mult),
                             reads=[bss, bsm], writes=[bt2])
                        k.op("dve", lambda e: e.scalar_tensor_tensor(out=t1[:, :], in0=t2[:, :], scalar=nlam, in1=t1[:, :], op0=ALU.mult, op1=ALU.add),
                             reads=[bt2, b_lam], writes=[bt1])
                        ot, bot = oat.next()
                        subln(t1[:, :], bt1, 128, ot[:, :], bot)
                        tok0 = u * 512 + qt * 128
                        k.dma("sp", oa_s[tok0:tok0 + 128, h * 128:(h + 1) * 128], ot[:, :], reads=[bot], writes=[bufs_d["oa"]])

        k.barrier()
        esC.close()
        esC = ExitStack()
        cur[0] = esC
        if DO_RET:
            Sown = sbc("Sown", [128, 4, 2048], F32)
            b_Sown = [Buf() for _ in range(4)]
            R = sbc("R", [128, 2048], F32)
            b_R = Buf()
            Eld = rot("Eld", [128, 2048], F32, 2)
            mskt = sbc("mskt", [128, 4], F32)
            b_msk = Buf()
            k.dma("sp", mskt[:], msk_d, writes=[b_msk])
            k.op("pool", lambda e: e.memset(R[:], 0.0), writes=[b_R])
            for G in range(16):
                u, jj = G // 4, G % 4
                if jj == 0:
                    k.op("dve", lambda e: e.tensor_scalar(out=Sown[:, u, :], in0=R[:, :], scalar1=mskt[:, 0:1], scalar2=None, op0=ALU.mult),
                         reads=[b_R, b_msk], writes=[b_Sown[u]])
                else:
                    k.op("dve", lambda e: e.scalar_tensor_tensor(out=Sown[:, u, :], in0=R[:, :], scalar=mskt[:, jj:jj + 1], in1=Sown[:, u, :],
                                                                op0=ALU.mult, op1=ALU.add),
                         reads=[b_R, b_msk, b_Sown[u]], writes=[b_Sown[u]])
                et, bet = Eld.next()
                k.dma("sp", et[:, :], E_g[u][jj * 128:(jj + 1) * 128, :], reads=[bg["E"][u]], writes=[bet])
                for h in range(4):
                    hs = slice(h * 512, (h + 1) * 512)
                    k.op("dve" if h % 2 == 0 else "pool",
                         (lambda e: e.scalar_tensor_tensor(out=R[:, hs], in0=R[:, hs], scalar=float(GAM[h] ** 512), in1=et[:, hs], op0=ALU.mult, op1=ALU.add))
                         if h % 2 == 0 else
                         (lambda e: e.tensor_scalar(out=R[:, hs], in0=R[:, hs], scalar1=float(GAM[h] ** 512), scalar2=None, op0=ALU.mult)),
                         reads=[b_R, bet], writes=[b_R])
                    if h % 2 == 1:
                        k.op("pool", lambda e: e.tensor_tensor(out=R[:, hs], in0=R[:, hs], in1=et[:, hs], op=ALU.add), reads=[b_R, bet], writes=[b_R])
            k.dma("sp", st_p.rearrange("h (a p) b -> p h a b", p=128), R[:, :].rearrange("p (h a b) -> p h a b", h=4, a=2),
                  reads=[b_R], writes=[bufs_d["out"]])

            S = sbc("S", [128, 2048], F32)
            b_S = Buf()
            Sb = sbc("Sb", [128, 2048], BF16)
            b_Sb = Buf()
            DM = sbc("DM", [128, 4, 128], F32)
            b_DM = Buf()
            qrc = rot("qrc", [128, 8, 128], BF16, 2)
            krc = rot("krc", [128, 8, 128], BF16, 2)
            krdc = rot("krdc", [128, D], BF16, 2)
            vrc = rot("vrc", [128, D], BF16, 2)
            grt = rot("grt", [128, D], F32, 2)
            ort = rot("ort", [128, D], F32, 2)
            qdc = rot("qdc", [128, 8], F32, 2)
            smr = rot("smr", [128, 128], BF16, 2)
            inn = rot("inn", [128, 256], F32, 2)
            bst = rot("bst", [128, 8], F32, 4)

            def ret_tile(tok0, nt, DMd, cross_fn, upd_fn):
                q_, bq = qrc.next()
                k_, bk = krc.next()
                kd, bkd = krdc.next()
                v_, bv = vrc.next()
                g_, bgr = grt.next()
                o_, bo = ort.next()
                qd, bqd = qdc.next()
                k.dma("sp", q_[:, :, 0:nt], qrT_s[:, :, tok0:tok0 + nt].rearrange("h p t -> p h t"), reads=[bufs_d["qrT"]], writes=[bq])
                k.dma("sp", k_[:, :, 0:nt], krT_s[:, :, tok0:tok0 + nt].rearrange("h p t -> p h t"), reads=[bufs_d["krT"]], writes=[bk])
                k.dma("sp", kd[0:nt, :], krd_s[tok0:tok0 + nt, :], reads=[bufs_d["krd"]], writes=[bkd])
                k.dma("sp", v_[0:nt, :], vr_s[tok0:tok0 + nt, :], reads=[bufs_d["vr"]], writes=[bv])
                k.dma("sp", g_[0:nt, :], gate_s[tok0:tok0 + nt, 0:D], reads=[bufs_d["gate"]], writes=[bgr])
                k.dma("sp", qd[0:nt, :], dec[tok0:tok0 + nt, :], writes=[bqd])
                k.dma("sp", DM[0:nt, :, 0:nt], DMd, writes=[b_DM])
                ops = (q_, bq, k_, bk, kd, bkd, v_, bv)
                crosses = cross_fn(ops)
                for h in range(4):
                    ps_, bps = pl[0]
                    for dt_ in range(2):
                        k.op("pe", lambda e: e.matmul(ps_[0:nt, 0:nt], lhsT=k_[:, h * 2 + dt_, 0:nt], rhs=q_[:, h * 2 + dt_, 0:nt],
                                                      start=(dt_ == 0), stop=(dt_ == 1)),
                             reads=[bk, bq], writes=[bps])
                    sm_, bsm_ = smr.next()
                    k.op("dve", lambda e: e.tensor_tensor(out=sm_[0:nt, 0:nt], in0=ps_[0:nt, 0:nt], in1=DM[0:nt, h, 0:nt], op=ALU.mult),
                         reads=[bps, b_DM], writes=[bsm_])
                    pi_, bpi = pl[1]
                    k.op("pe", lambda e: e.matmul(pi_[0:nt, 0:256], lhsT=sm_[0:nt, 0:nt], rhs=v_[0:nt, h * 256:(h + 1) * 256], start=True, stop=True),
                         reads=[bsm_, bv], writes=[bpi])
                    in_, bin_ = inn.next()
                    k.op("act", lambda e: e.copy(out=in_[0:nt, :], in_=pi_[0:nt, 0:256]), reads=[bpi], writes=[bin_])
                    pc_, bpc = crosses[h]
                    k.op("dve", lambda e: e.scalar_tensor_tensor(out=o_[0:nt, h * 256:(h + 1) * 256], in0=pc_[0:nt, 0:256], scalar=qd[0:nt, 4 + h:5 + h],
                                                                in1=in_[0:nt, :], op0=ALU.mult, op1=ALU.add),
                         reads=[bpc, bqd, bin_], writes=[bo])
                    if upd_fn is not None:
                        upd_fn(h, ops)
                    st_, bst_ = bst.next()
                    mv, bmv = bst.next()
                    osl = o_[0:nt, h * 256:(h + 1) * 256]
                    k.op("dve", lambda e: e.bn_stats(out=st_[0:nt, 0:6], in_=osl), reads=[bo], writes=[bst_])
                    k.op("dve", lambda e: e.bn_aggr(out=mv[0:nt, 0:2], in_=st_[0:nt, 0:6]), reads=[bst_], writes=[bmv])
                    k.op("act", lambda e: e.activation(out=mv[0:nt, 2:3], in_=mv[0:nt, 1:2], func=AF.Sqrt, bias=eps5_t[0:nt, 0:1]),
                         reads=[bmv, b_eps], writes=[bmv])
                    k.op("dve", lambda e: e.reciprocal(out=mv[0:nt, 3:4], in_=mv[0:nt, 2:3]), reads=[bmv], writes=[bmv])
                    k.op("dve", lambda e: e.scalar_tensor_tensor(out=mv[0:nt, 4:5], in0=mv[0:nt, 0:1], scalar=-1.0, in1=mv[0:nt, 3:4],
                                                                op0=ALU.mult, op1=ALU.mult),
                         reads=[bmv], writes=[bmv])
                    k.op("act", lambda e: e.activation(out=osl, in_=osl, func=AF.Identity, scale=mv[0:nt, 3:4], bias=mv[0:nt, 4:5]),
                         reads=[bmv, bo], writes=[bo])
                    k.op("pool", lambda e: e.tensor_tensor(out=osl, in0=osl, in1=rng[0:nt, h * 256:(h + 1) * 256], op=ALU.mult),
                         reads=[bo, b_rng], writes=[bo])
                    k.op("pool", lambda e: e.tensor_tensor(out=osl, in0=osl, in1=g_[0:nt, h * 256:(h + 1) * 256], op=ALU.mult),
                         reads=[bo, bgr], writes=[bo])
                k.dma("sp", or_s[tok0:tok0 + nt, :], o_[0:nt, :], reads=[bo], writes=[bufs_d["or"]])

            for u in range(4):
                k.op("act", lambda e: e.copy(out=S[:, :], in_=Sown[:, u, :]), reads=[b_Sown[u]], writes=[b_S])
                for n in range(4):
                    tok0 = u * 512 + n * 128
                    k.op("act", lambda e: e.copy(out=Sb[:, :], in_=S[:, :]), reads=[b_S], writes=[b_Sb])

                    def cross_fn(ops):
                        q_, bq = ops[0], ops[1]
                        res = []
                        for h in range(4):
                            pc_, bpc = pmm.items[h]
                            for dt_ in range(2):
                                k.op("pe", lambda e: e.matmul(pc_[:, 0:256], lhsT=q_[:, h * 2 + dt_, :], rhs=Sb[:, (h * 2 + dt_) * 256:(h * 2 + dt_ + 1) * 256],
                                                              start=(dt_ == 0), stop=(dt_ == 1)),
                                     reads=[bq, b_Sb], writes=[bpc])
                            res.append((pc_, bpc))
                        return res

                    def upd_fn(h, ops):
                        kd, bkd, v_, bv = ops[4], ops[5], ops[6], ops[7]
                        pd, bpd = pmm.items[h]
                        for dt_ in range(2):
                            k.op("pe", lambda e: e.matmul(pd[:, dt_ * 256:(dt_ + 1) * 256], lhsT=kd[:, h * 256 + dt_ * 128:h * 256 + (dt_ + 1) * 128],
                                                          rhs=v_[:, h * 256:(h + 1) * 256], start=True, stop=True),
                                 reads=[bkd, bv], writes=[bpd])
                        hs = slice(h * 512, (h + 1) * 512)
                        k.op("dve", lambda e: e.scalar_tensor_tensor(out=S[:, hs], in0=S[:, hs], scalar=float(GAM[h] ** 128), in1=pd[:, :],
                                                                    op0=ALU.mult, op1=ALU.add),
                             reads=[bpd, b_S], writes=[b_S])

                    ret_tile(tok0, 128, dmp_d, cross_fn, upd_fn)

            qz = sbc("qz", [128, 16, 8, 64], BF16)
            b_qz = Buf()
            k.op("pool", lambda e: e.memset(qz[:], 0.0), writes=[b_qz])
            seqm = sbc("seqm_t", [64, 16], F32)
            b_seqm = Buf()
            k.dma("sp", seqm[:], seqm_d, writes=[b_seqm])
            Stt = rot("Stt", [128, 2048], F32, 2)
            Snew = rot("Snew", [128, 2048], F32, 2)
            kdz = rot("kdz", [64, D], BF16, 2)

            def cross_s(ops):
                q_, bq, kd, bkd, v_, bv = ops[0], ops[1], ops[4], ops[5], ops[6], ops[7]
                for i in range(16):
                    k.op("dve", lambda e: e.tensor_copy(out=qz[:, i, :, i * 4:(i + 1) * 4], in_=q_[:, :, i * 4:(i + 1) * 4]), reads=[bq], writes=[b_qz])
                for i in range(16):
                    st_, bst_ = Stt.next()
                    k.dma("sp", st_[:, :].rearrange("p (h a b) -> p h a b", h=4, a=2), st_in[i].rearrange("h (a p) b -> p h a b", p=128), writes=[bst_])
                    k.op("act", lambda e: e.copy(out=Sb[:, :], in_=st_[:, :]), reads=[bst_], writes=[b_Sb])
                    for h in range(4):
                        pc_, bpc = pmm.items[h]
                        for dt_ in range(2):
                            k.op("pe", lambda e: e.matmul(pc_[0:64, 0:256], lhsT=qz[:, i, h * 2 + dt_, :], rhs=Sb[:, (h * 2 + dt_) * 256:(h * 2 + dt_ + 1) * 256],
                                                          start=(i == 0 and dt_ == 0), stop=(i == 15 and dt_ == 1)),
                                 reads=[b_qz, b_Sb], writes=[bpc])
                    kz, bkz = kdz.next()
                    k.op("dve", lambda e: e.tensor_scalar(out=kz[0:64, :], in0=kd[0:64, :], scalar1=seqm[0:64, i:i + 1], scalar2=None, op0=ALU.mult),
                         reads=[bkd, b_seqm], writes=[bkz])
                    sn, bsn = Snew.next()
                    for h in range(4):
                        pd, bpd = pl[h % 2]
                        for dt_ in range(2):
                            k.op("pe", lambda e: e.matmul(pd[:, dt_ * 256:(dt_ + 1) * 256], lhsT=kz[0:64, h * 256 + dt_ * 128:h * 256 + (dt_ + 1) * 128],
                                                          rhs=v_[0:64, h * 256:(h + 1) * 256], start=True, stop=True),
                                 reads=[bkz, bv], writes=[bpd])
                        hs = slice(h * 512, (h + 1) * 512)
                        k.op("dve", lambda e: e.scalar_tensor_tensor(out=sn[:, hs], in0=st_[:, hs], scalar=float(GAM[h] ** 4), in1=pd[:, :],
                                                                    op0=ALU.mult, op1=ALU.add),
                             reads=[bpd, bst_], writes=[bsn])
                    k.dma("sp", st_s[i].rearrange("h (a p) b -> p h a b", p=128), sn[:, :].rearrange("p (h a b) -> p h a b", h=4, a=2),
                          reads=[bsn], writes=[bufs_d["out"]])
                return [pmm.items[h] for h in range(4)]

            ret_tile(NPT, NST, dms_d, cross_s, None)
        k.barrier()
        esC.close()

        esS = ExitStack()
        cur[0] = esS
        if DO_SATT:
            ptt = sbc("ptt", [128, 256], I32)
            iot = sbc("iot", [128, 1], I32)
            idx = sbc("idx", [128, 256], I32)
            b_idx = Buf()
            k.dma("sp", ptt[:], pt_d.partition_broadcast(128), writes=[b_idx])
            k.op("pool", lambda e: e.iota(iot[:], pattern=[[0, 1]], base=0, channel_multiplier=1), writes=[b_idx])
            k.op("dve", lambda e: e.tensor_scalar(out=idx[:], in0=ptt[:], scalar1=128, scalar2=iot[:, 0:1], op0=ALU.mult, op1=ALU.add),
                 reads=[b_idx], writes=[b_idx])
            qTs = sbc("qTs", [128, 8, 64], BF16)
            kTn = sbc("kTn", [128, 8, 64], BF16)
            b_qk = Buf()
            k.dma("sp", qTs[:], qT_s[:, :, NPT:NTOK].rearrange("h p t -> p h t"), reads=[bufs_d["qT"]], writes=[b_qk])
            k.dma("sp", kTn[:], kTs_s.rearrange("h p t -> p h t"), reads=[bufs_d["kTs"]], writes=[b_qk])
            Qblk = sbc("Qblk", [128, 8, 16, 8], BF16)
            k.op("pool", lambda e: e.memset(Qblk[:], 0.0), writes=[b_qk])
            for c in range(2):
                k.op("dve", lambda e: e.tensor_copy(out=Qblk[c * 64:(c + 1) * 64, :, :, c * 4:(c + 1) * 4],
                                                    in_=qTs[c * 64:(c + 1) * 64, :, :].rearrange("p h (i q) -> p h i q", q=4)),
                     reads=[b_qk], writes=[b_qk])
            satab = sbc("satab_t", [64, 8 * 128 + 8], F32)
            b_sat = Buf()
            k.dma("sp", satab[:], satab_d, writes=[b_sat])
            tri = sbc("tri_t", [4, 64], F32)
            k.dma("sp", tri[:], tri_d, writes=[b_sat])
            csel = sbc("csel", [64, 1], F32)
            k.op("dve", lambda e: e.scalar_tensor_tensor(out=csel[:], in0=satab[:, 1025:1026], scalar=nlam[0:64, :], in1=satab[:, 1024:1025],
                                                        op0=ALU.mult, op1=ALU.add), reads=[b_sat, b_lam], writes=[b_sat])
            pg32 = rot("pg32", [128, D], F32, 4)
            pgb = rot("pgb", [128, D], BF16, 3)
            KTp = rot("KTp", [128, 8, 128], BF16, 2)
            PTs = rot("PTs", [128, 64], BF16, 3)
            vnew = rot("vnew", [4, D], BF16, 2)
            msk_sb = sbc("msk_sb", [64, D], F32)
            b_msb = Buf()
            coef = rot("coef", [64, 2], F32, 2)
            o4 = rot("o4", [4, D], F32, 2)
            accb = [pmm.items[2], pmm.items[3]]
            outb = [pmm.items[0], pmm.items[1]]
            ps_sc, b_sc = pl[0]
            ps_l, b_l = pl[1]
            flip = 0
            for i in range(16 if DBG_SA >= 1 else 0):
                for pg in range(17):
                    last = (pg == 16)
                    if not last:
                        n = i * 16 + pg
                        kp, bkp = pg32.next()
                        k.dma("pool", None, None, reads=[b_idx], writes=[bkp],
                              fn=lambda e: e.indirect_dma_start(out=kp[:, :], out_offset=None, in_=cache_k2,
                                                                in_offset=bass.IndirectOffsetOnAxis(ap=idx[:, n:n + 1], axis=0)))
                        vp, bvp = pg32.next()
                        k.dma("pool", None, None, reads=[b_idx], writes=[bvp],
                              fn=lambda e: e.indirect_dma_start(out=vp[:, :], out_offset=None, in_=cache_v2,
                                                                in_offset=bass.IndirectOffsetOnAxis(ap=idx[:, n:n + 1], axis=0)))
                        kb_, bkb = pgb.next()
                        k.op("dve", lambda e: e.tensor_copy(out=kb_[:, :], in_=kp[:, :]), reads=[bkp], writes=[bkb])
                        ptT, bptT = ptr.next()
                        for c in range(8):
                            k.op("pe", lambda e: e.transpose(out=ptT[:, c, :], in_=kb_[:, c * 128:(c + 1) * 128], identity=idb[:, :]),
                                 reads=[bkb, b_id], writes=[bptT])
                        kt_, bkt = KTp.next()
                        k.op("act", lambda e: e.copy(out=kt_[:, :, :], in_=ptT[:, :, :]), reads=[bptT], writes=[bkt])
                        vb_, bvb = pgb.next()
                        k.op("act" if flip else "dve",
                             (lambda e: e.copy(out=vb_[:, :], in_=vp[:, :])) if flip else (lambda e: e.tensor_copy(out=vb_[:, :], in_=vp[:, :])),
                             reads=[bvp], writes=[bvb])
                        flip ^= 1
                        nk = 128
                        for h in range(8 if (DBG_SA >= 2 and DBG_SV != 2) else 0):
                            k.op("pe", lambda e: e.matmul(ps_sc[0:128, h * 8:(h + 1) * 8], lhsT=kt_[:, h, :], rhs=Qblk[:, h, i, :], start=True, stop=True),
                                 reads=[bkt, b_qk], writes=[b_sc])
                        vrows = vb_
                    else:
                        nk = 4
                        for h in range(8 if (DBG_SA >= 2 and DBG_SV != 1) else 0):
                            k.op("pe", lambda e: e.matmul(ps_sc[0:4, h * 8:(h + 1) * 8], lhsT=kTn[:, h, i * 4:(i + 1) * 4], rhs=Qblk[:, h, i, :], start=True, stop=True),
                                 reads=[b_qk], writes=[b_sc])
                        vrows, bvb = vnew.next()
                        k.dma("sp", vrows[0:4, :], vs_s[i * 4:(i + 1) * 4, :], reads=[bufs_d["vs"]], writes=[bvb])
                    if DBG_SA < 3:
                        continue
                    p_, bp_ = PTs.next()
                    k.op("act", lambda e: e.activation(out=p_[0:nk, :], in_=ps_sc[0:nk, 0:64], func=AF.Exp), reads=[b_sc], writes=[bp_])
                    if last:
                        k.op("dve", lambda e: e.tensor_tensor(out=p_[0:4, :], in0=p_[0:4, :], in1=tri[0:4, :], op=ALU.mult), reads=[b_sat], writes=[bp_])
                    if DBG_SA < 4:
                        continue
                    for hf in range(2):
                        k.op("pe", lambda e: e.matmul(accb[hf][0][0:64, :], lhsT=p_[0:nk, 0:64], rhs=vrows[0:nk, hf * 512:(hf + 1) * 512],
                                                      start=(pg == 0), stop=last),
                             reads=[bp_, bvb], writes=[accb[hf][1]])
                    k.op("pe", lambda e: e.matmul(ps_l[0:64, 0:1], lhsT=p_[0:nk, 0:64], rhs=ones_b[0:nk, 0:1], start=(pg == 0), stop=last),
                         reads=[bp_, b_eps], writes=[b_l])
                if DBG_SA < 5:
                    continue
                cf, bcf = coef.next()
                k.op("dve", lambda e: e.reciprocal(out=cf[:, 0:1], in_=ps_l[0:64, 0:1]), reads=[b_l], writes=[bcf])
                k.op("dve", lambda e: e.tensor_tensor(out=cf[:, 1:2], in0=cf[:, 0:1], in1=csel[:, 0:1], op=ALU.mult), reads=[b_sat], writes=[bcf])
                for hf in range(2):
                    k.op("dve", lambda e: e.scalar_tensor_tensor(out=msk_sb[:, hf * 512:(hf + 1) * 512], in0=accb[hf][0][0:64, :], scalar=cf[:, 1:2],
                                                                in1=satab[:, hf * 512:(hf + 1) * 512], op0=ALU.mult, op1=ALU.mult),
                         reads=[accb[hf][1], bcf, b_sat], writes=[b_msb])
                o_, bo_ = o4.next()
                for hf in range(2):
                    k.op("pe", lambda e: e.matmul(outb[hf][0][0:4, :], lhsT=satab[:, 1026:1030], rhs=msk_sb[:, hf * 512:(hf + 1) * 512], start=True, stop=True),
                         reads=[b_msb, b_sat], writes=[outb[hf][1]])
                    k.op("act", lambda e: e.copy(out=o_[0:4, hf * 512:(hf + 1) * 512], in_=outb[hf][0][0:4, :]), reads=[outb[hf][1]], writes=[bo_])
                k.dma("sp", attn_s[i * 4:(i + 1) * 4, :], o_[0:4, :], reads=[bo_], writes=[bufs_d["attn"]])
            at_ = sbc("at_", [64, D], F32)
            b_at = Buf()
            k.dma("sp", at_[:, :], attn_s[:, :], reads=[bufs_d["attn"]], writes=[b_at])
            ao_ = sbc("ao_", [64, D], F32)
            b_ao = Buf()
            for h in range(8):
                subln(at_[0:64, h * 128:(h + 1) * 128], b_at, 64, ao_[0:64, h * 128:(h + 1) * 128], b_ao)
            k.dma("sp", oa_s[NPT:NTOK, :], ao_[:, :], reads=[b_ao], writes=[bufs_d["oa"]])
        k.barrier()
        esS.close()

        esD = ExitStack()
        mk_work(esD, "d", False)
        oat_d = Rot([(esD.enter_context(nc.sbuf_tensor("oat_d%d" % i, [128, D], F32)), Buf()) for i in range(1)])
        ort_d = Rot([(esD.enter_context(nc.sbuf_tensor("ort_d%d" % i, [128, D], F32)), Buf()) for i in range(1)])
        gat_d = Rot([(esD.enter_context(nc.sbuf_tensor("gat_d%d" % i, [128, 2 * D], F32)), Buf()) for i in range(1)])
        nf_t = esD.enter_context(nc.sbuf_tensor("nf_t", [128, D], F32))
        b_nf = Buf()
        k.dma("sp", nf_t[:], nrm[3:4, :].partition_broadcast(128), writes=[b_nf])

        def phase_d(tiles, y_out, yoff):
            for ti, (tok0, nt) in enumerate(tiles):
                k.dma("sp", W.xb[0:nt, ti, :], xmid_s[tok0:tok0 + nt, :], reads=[bufs_d["xmid"]], writes=[W.b_x[ti]])
                oa_, boa = oat_d.next()
                or_, bor = ort_d.next()
                ga_, bga = gat_d.next()
                k.dma("sp", oa_[0:nt, :], oa_s[tok0:tok0 + nt, :], reads=[bufs_d["oa"]], writes=[boa])
                k.dma("sp", or_[0:nt, :], or_s[tok0:tok0 + nt, :], reads=[bufs_d["or"]], writes=[bor])
                k.dma("sp", ga_[0:nt, :], gate_s[tok0:tok0 + nt, D:3 * D], reads=[bufs_d["gate"]], writes=[bga])
                k.op("dve", lambda e: e.tensor_tensor(out=oa_[0:nt, :], in0=oa_[0:nt, :], in1=ga_[0:nt, 0:D], op=ALU.mult), reads=[bga], writes=[boa])
                k.op("pool", lambda e: e.tensor_tensor(out=or_[0:nt, :], in0=or_[0:nt, :], in1=ga_[0:nt, D:2 * D], op=ALU.mult), reads=[bga], writes=[bor])
                mb, bmb = W.bf16t.next()
                k.op("dve", lambda e: e.tensor_tensor(out=mb[0:nt, :], in0=oa_[0:nt, :], in1=or_[0:nt, :], op=ALU.add), reads=[boa, bor], writes=[bmb])
                transpose_to(mb, bmb, nt, 8, lambda c0, c1: W.hT[:, c0:c1, ti * 128:ti * 128 + nt], W.b_hT[ti])

            def body_o(blk, wt, bw):
                for ti, (tok0, nt) in enumerate(tiles):
                    pt, bp = mm_block(lambda c: W.hT[:, c, ti * 128:ti * 128 + nt], 8, wt, bw, 512, nt, [W.b_hT[ti]])
                    st, bst_ = W.half.next()
                    k.op("dve", lambda e: e.tensor_tensor(out=st[0:nt, :], in0=pt[0:nt, :], in1=mod[0:nt, 5 * D + blk * 512:5 * D + (blk + 1) * 512], op=ALU.mult),
                         reads=[bp, b_mod], writes=[bst_])
                    k.op("pool", lambda e: e.tensor_tensor(out=W.xb[0:nt, ti, blk * 512:(blk + 1) * 512], in0=W.xb[0:nt, ti, blk * 512:(blk + 1) * 512],
                                                          in1=st[0:nt, :], op=ALU.add),
                         reads=[bst_, W.b_x[ti]], writes=[W.b_x[ti]])
            stream_w([(lambda blk=blk: load_w(w_out, 8, [(blk * 512, 512, 0)], W.wk8, ("wo", blk))) for blk in range(2)], body_o)
            ffn(tiles, f2_in, f2_out, 6, 7, 8, "f2")
            for ti, (tok0, nt) in enumerate(tiles):
                sm, bsm = W.small.next()
                jt, bj = W.f32t.next()
                xt = W.xb[0:nt, ti, :]
                k.op("act", lambda e: e.activation(out=jt[0:nt, :], in_=xt, func=AF.Square, accum_out=sm[0:nt, 0:1]), reads=[W.b_x[ti]], writes=[bj, bsm])
                k.op("act", lambda e: e.activation(out=sm[0:nt, 1:2], in_=sm[0:nt, 0:1], func=AF.Sqrt, scale=1.0 / D, bias=eps_t[0:nt, 0:1]),
                     reads=[bsm, b_eps], writes=[bsm])
                k.op("dve", lambda e: e.reciprocal(out=sm[0:nt, 2:3], in_=sm[0:nt, 1:2]), reads=[bsm], writes=[bsm])
                k.op("dve", lambda e: e.scalar_tensor_tensor(out=jt[0:nt, :], in0=xt, scalar=sm[0:nt, 2:3], in1=nf_t[0:nt, :], op0=ALU.mult, op1=ALU.mult),
                     reads=[W.b_x[ti], bsm, b_nf], writes=[bj])
                k.dma("sp", y_out[tok0 - yoff:tok0 - yoff + nt, :], jt[0:nt, :], reads=[bj], writes=[bufs_d["out"]])

        phase_d([(NPT, NST)], y_s, NPT)
        compute_mod(0, 128)
        for u in range(4):
            phase_d([(u * 512 + i * 128, 128) for i in range(4)], y_p, 0)
        k.finish_all()
        esD.close()
    print("instructions", k.n_inst, "waits", k.n_wait)
    return nc


def host_tables(j):
    pos = np.concatenate([np.concatenate([np.arange(512) + 512 * (4 * u + j) for u in range(4)]),
                          np.tile(2048 + np.arange(4), 16)]).astype(np.float32)
    inv = (10000.0 ** (-np.arange(32, dtype=np.float32) / 32)).astype(np.float32)
    ang = pos[:, None] * inv[None, :]
    c, s = np.cos(ang), np.sin(ang)
    cc = np.concatenate([c, c], 1)
    ss = np.concatenate([s, s], 1)
    ropek = np.stack([cc, ss], 1).astype(np.float32)
    ropeq = (ropek * 0.125).astype(np.float32)
    angle = (1.0 / (10000.0 ** np.linspace(0.0, 1.0, 128, dtype=np.float32))).astype(np.float32)
    ang2 = pos[:, None] * angle[None, :]
    c2 = np.repeat(np.cos(ang2), 2, axis=1)
    s2 = np.repeat(np.sin(ang2), 2, axis=1)
    retq = np.stack([c2, s2], 1).astype(np.float32)
    retk = (retq / 16.0).astype(np.float32)
    lg = np.log1p(-np.exp2(-5.0 - np.arange(4, dtype=np.float32))).astype(np.float32)
    idx = np.concatenate([np.tile(np.arange(128), 16), np.tile(np.arange(4), 16)]).astype(np.float32)
    L = np.concatenate([np.full(NPT, 128.0), np.full(NST, 4.0)]).astype(np.float32)
    kdec = np.exp((L - 1.0 - idx)[:, None] * lg[None, :])
    qdec = np.exp((idx + 1.0)[:, None] * lg[None, :])
    dec = np.concatenate([kdec, qdec], 1).astype(np.float32)
    kp = np.arange(128)[:, None, None]
    rel = np.arange(16)[None, :, None]
    qi = np.arange(512)[None, None, :]
    maskd = (128 * (rel - 4 * j) + kp <= qi).astype(np.float32)
    msk = np.zeros((128, 4), np.float32)
    msk[:, j] = 1.0
    sI = np.arange(128)[:, None, None].astype(np.float32)
    lI = np.arange(128)[None, None, :].astype(np.float32)
    dmp = np.where(lI >= sI, np.exp(np.maximum(lI - sI, 0.0) * lg[None, :, None]), 0.0).astype(np.float32)
    s6 = np.arange(64)[:, None, None]
    l6 = np.arange(64)[None, None, :]
    dms = np.where((l6 >= s6) & (l6 // 4 == s6 // 4), np.exp(np.maximum(l6 - s6, 0).astype(np.float32) * lg[None, :, None]), 0.0).astype(np.float32)
    seqm = (np.arange(64)[:, None] // 4 == np.arange(16)[None, :]).astype(np.float32)
    row = np.arange(64)
    rh, rc, rq = row // 8, (row // 4) % 2, row % 4
    satab = np.zeros((64, 1032), np.float32)
    for r in range(64):
        satab[r, rh[r] * 128:(rh[r] + 1) * 128] = 1.0
        satab[r, 1024] = 1.0 if rc[r] == 0 else 0.0
        satab[r, 1025] = 1.0 if rc[r] == 1 else 0.0
        satab[r, 1026 + rq[r]] = 1.0
    tri = (np.arange(4)[:, None] <= (np.arange(64)[None, :] % 4)).astype(np.float32)
    return dict(ropeq=ropeq, ropek=ropek, retq=retq, retk=retk, dec=dec, maskd=maskd, msk=msk, dmp=dmp, dms=dms, seqm=seqm, satab=satab, tri=tri)


def make_in_maps(inp, npool=NPOOL_FULL):
    f = lambda a: np.ascontiguousarray(np.asarray(a, dtype=np.float32))
    xpr = f(inp["x_prompt"])
    xsm = f(inp["x_sample"])
    cp = f(inp["c_prompt"])
    csm = f(inp["c_sample"])
    nrm = np.stack([f(inp["norm_ffn1"])[0], f(inp["norm_mix"])[0], f(inp["norm_ffn2"])[0],
                    f(inp["norm_final"]), f(inp["ret_norm_g"])[0]], 0)
    shared = dict(ada_w=f(inp["ada_w"])[0], ada_b=f(inp["ada_b"]), nrm=nrm,
                  ffn1_w_in=f(inp["ffn1_w_in"])[0], ffn1_w_out=f(inp["ffn1_w_out"])[0],
                  ffn2_w_in=f(inp["ffn2_w_in"])[0], ffn2_w_out=f(inp["ffn2_w_out"])[0],
                  w_in=f(inp["w_in"])[0], w_out=f(inp["w_out"])[0], ident=np.eye(128, dtype=np.float32),
                  lam=np.stack([f(inp["lam_q1"])[0], f(inp["lam_k1"])[0], f(inp["lam_q2"])[0], f(inp["lam_k2"])[0]], 0),
                  subln=f(inp["subln_g"]))
    sret = inp["state_ret"]
    ck = np.asarray(inp["cache_k"], dtype=np.float32).reshape(-1, D)
    cv = np.asarray(inp["cache_v"], dtype=np.float32).reshape(-1, D)
    ptab = np.asarray(inp["page_table"], dtype=np.int32)
    maps = []
    for c in range(8):
        s, j = c // 4, c % 4
        xg = xpr[s].reshape(16, 512, D)
        m = dict(shared)
        m["xp"] = np.ascontiguousarray(np.concatenate([xg[4 * u + j] for u in range(4)], 0))
        m["xs"] = np.ascontiguousarray(xsm[16 * c:16 * c + 16].reshape(NST, D))
        m["ctok"] = np.ascontiguousarray(np.concatenate([np.broadcast_to(cp[s], (128, D)),
                                                         np.repeat(csm[16 * c:16 * c + 16], 4, axis=0)], 0))
        m["pt"] = np.ascontiguousarray(ptab[16 * c:16 * c + 16].reshape(1, 256))
        m["cache_k"] = ck
        m["cache_v"] = cv
        m["st_in"] = np.ascontiguousarray(np.asarray(sret[0, 16 * c:16 * c + 16], dtype=np.float32))
        m.update(host_tables(j))
        maps.append(m)
    return maps


def assemble(res):
    y_prompt = np.zeros((2, 8192, D), np.float32)
    k_prompt = np.zeros((1, 2, 8192, 16, 64), np.float32)
    v_prompt = np.zeros((1, 2, 8192, 8, 128), np.float32)
    y_sample = np.zeros((128, 4, D), np.float32)
    k_sample = np.zeros((1, 128, 4, 16, 64), np.float32)
    v_sample = np.zeros((1, 128, 4, 8, 128), np.float32)
    st_p = np.zeros((1, 2, 4, 256, 256), np.float32)
    st_s = np.zeros((1, 128, 4, 256, 256), np.float32)
    for c in range(8):
        s, j = c // 4, c % 4
        r = res[c]
        for u in range(4):
            g = 4 * u + j
            sl = slice(512 * g, 512 * (g + 1))
            y_prompt[s, sl] = r["y_p"][u * 512:(u + 1) * 512]
            k_prompt[0, s, sl] = r["k_p"][u * 512:(u + 1) * 512].reshape(512, 16, 64)
            v_prompt[0, s, sl] = r["v_p"][u * 512:(u + 1) * 512].reshape(512, 8, 128)
        y_sample[16 * c:16 * c + 16] = r["y_s"].reshape(16, 4, D)
        k_sample[0, 16 * c:16 * c + 16] = r["k_s"].reshape(16, 4, 16, 64)
        v_sample[0, 16 * c:16 * c + 16] = r["v_s"].reshape(16, 4, 8, 128)
        st_s[0, 16 * c:16 * c + 16] = r["st_s"]
        if j == 0:
            st_p[0, s] = r["st_p"]
    return (y_prompt, y_sample, k_prompt, v_prompt, st_p, k_sample, v_sample, st_s)


_NC = {}


def kernel(**inputs):
    if "nc" not in _NC:
        _NC["nc"] = build(npool=int(np.asarray(inputs["cache_k"]).shape[1]))
    maps = make_in_maps(inputs)
    res = run_bass_kernel_spmd(_NC["nc"], maps, core_ids=list(range(8)))
    return assemble(res.results)
```
